# Optimizing a Trainium2 kernel written in Bass

```python
import jax, jax.numpy as jnp
from jax import lax
import numpy as np

D_MODEL = 1024
BATCH = 16
SEQ = 2048
DEPTH = 1
DEC_BATCH = 128
DEC_SEQ = 1
PAST_LEN = 16384
PAGE_SIZE = 128

D_MIX = D_MODEL
D_A = D_MIX // 2
N_HEADS_A = 8
HEAD_DIM_A = D_A // N_HEADS_A
CHUNK = 128
N_HEADS_B = 8
QK_NOPE_DIM = 64
QK_ROPE_DIM = 32
V_HEAD_DIM = 64
D_B = N_HEADS_B * V_HEAD_DIM
Q_RANK = 384
KV_RANK = 256
D_IN = 2 * D_A + Q_RANK + KV_RANK + QK_ROPE_DIM
D_FF = 2816
CONV_W = 3
ROPE_THETA = 10000.0
EPS = 1e-6
Q_BLOCK = 128
ATTN_SCALE = (QK_NOPE_DIM + QK_ROPE_DIM) ** -0.5

kernel_name = 'hybrid_gmlp_mla_convffn_step'


def rmsnorm(x, g):
    xf = x.astype(jnp.float32)
    r = lax.rsqrt(jnp.mean(xf * xf, axis=-1, keepdims=True) + EPS)
    return (xf * r).astype(x.dtype) * g


def rope(x, pos):
    half = x.shape[-1] // 2
    inv = ROPE_THETA ** (-jnp.arange(half, dtype=jnp.float32) / half)
    ang = pos.astype(jnp.float32)[:, None] * inv
    ang = ang.reshape(ang.shape[:1] + (1,) * (x.ndim - 3) + ang.shape[1:])
    cos = jnp.cos(ang).astype(x.dtype)
    sin = jnp.sin(ang).astype(x.dtype)
    x1, x2 = x[..., :half], x[..., half:]
    return jnp.concatenate([x1 * cos - x2 * sin, x1 * sin + x2 * cos], axis=-1)


def spatial_gate(u, v, w_spatial, b_spatial, chunk_len):
    B, T, _ = v.shape
    L = chunk_len
    vc = v.reshape(B, T // L, L, N_HEADS_A, HEAD_DIM_A)
    ws = w_spatial[:, :L, :L] * jnp.tril(jnp.ones((L, L), v.dtype))
    s = jnp.einsum('hts,bcshd->bcthd', ws, vc) + b_spatial[:, :L].T[None, None, :, :, None]
    return u * s.reshape(B, T, D_A)


def mla_prompt(q_nope, q_rope, c_kv, k_rope, w_uk, w_uv):
    B, S = c_kv.shape[:2]
    nb = S // Q_BLOCK
    k_nope = jnp.einsum('bsr,rhd->bshd', c_kv, w_uk)
    val = jnp.einsum('bsr,rhd->bshd', c_kv, w_uv)
    kpos = jnp.arange(S)
    qn = q_nope.reshape(B, nb, Q_BLOCK, N_HEADS_B, QK_NOPE_DIM).swapaxes(0, 1)
    qr = q_rope.reshape(B, nb, Q_BLOCK, N_HEADS_B, QK_ROPE_DIM).swapaxes(0, 1)

    def one_block(args):
        i, qn_b, qr_b = args
        s = (jnp.einsum('bqhd,bkhd->bhqk', qn_b, k_nope)
             + jnp.einsum('bqhd,bkd->bhqk', qr_b, k_rope)).astype(jnp.float32) * ATTN_SCALE
        qpos = i * Q_BLOCK + jnp.arange(Q_BLOCK)
        s = jnp.where(kpos[None, :] <= qpos[:, None], s, -jnp.inf)
        p = jax.nn.softmax(s, axis=-1).astype(val.dtype)
        return jnp.einsum('bhqk,bkhd->bqhd', p, val)

    o = lax.map(one_block, (jnp.arange(nb), qn, qr))
    return o.swapaxes(0, 1).reshape(B, S, D_B)


def mla_sample(q_nope, q_rope, c_new, kr_new, c_past, kr_past, w_uk, w_uv):
    B, T = c_new.shape[:2]
    P = c_past.shape[1]
    q_lat = jnp.einsum('bthd,rhd->bthr', q_nope, w_uk)
    s_past = jnp.einsum('bthr,bkr->bhtk', q_lat, c_past) + jnp.einsum('bthd,bkd->bhtk', q_rope, kr_past)
    s_new = jnp.einsum('bthr,bkr->bhtk', q_lat, c_new) + jnp.einsum('bthd,bkd->bhtk', q_rope, kr_new)
    s = jnp.concatenate([s_past, s_new], axis=-1).astype(jnp.float32) * ATTN_SCALE
    causal = jnp.arange(T)[None, :] <= jnp.arange(T)[:, None]
    mask = jnp.concatenate([jnp.ones((T, P), bool), causal], axis=-1)
    s = jnp.where(mask, s, -jnp.inf)
    p = jax.nn.softmax(s, axis=-1).astype(c_new.dtype)
    o_lat = (jnp.einsum('bhtk,bkr->bthr', p[..., :P], c_past)
             + jnp.einsum('bhtk,bkr->bthr', p[..., P:], c_new))
    o = jnp.einsum('bthr,rhd->bthd', o_lat, w_uv)
    return o.reshape(B, T, D_B)


def conv_ffn(x, conv_prev, g_ffn, w_up, w_conv, b_conv, w_down):
    h = rmsnorm(x, g_ffn)
    up = h @ w_up
    T = up.shape[1]
    xp = jnp.concatenate([conv_prev, up], axis=1)
    c = b_conv
    for k in range(CONV_W):
        c = c + xp[:, k:k + T] * w_conv[k]
    gate, val = c[..., :D_FF], c[..., D_FF:]
    y = (jax.nn.silu(gate) * val) @ w_down
    return y, xp[:, -(CONV_W - 1):]


def layer(x, pos, conv_prev, chunk_len, attend, p):
    h = rmsnorm(x, p['g_mix'])
    proj = h @ p['w_in']
    o1, o2 = D_A, 2 * D_A
    o3 = o2 + Q_RANK
    o4 = o3 + KV_RANK
    u = jax.nn.gelu(proj[..., :o1], approximate=False)
    v = rmsnorm(jax.nn.gelu(proj[..., o1:o2], approximate=False), p['g_sgu'])
    c_q = rmsnorm(proj[..., o2:o3], p['g_q'])
    c_kv = rmsnorm(proj[..., o3:o4], p['g_kv'])
    k_rope = rope(proj[..., o4:], pos)
    q = jnp.einsum('bsr,rhd->bshd', c_q, p['w_uq'])
    q_nope = q[..., :QK_NOPE_DIM]
    q_rope = rope(q[..., QK_NOPE_DIM:], pos)
    out_a = rmsnorm(spatial_gate(u, v, p['w_spatial'], p['b_spatial'], chunk_len), p['g_out_a'])
    out_b = rmsnorm(attend(q_nope, q_rope, c_kv, k_rope), p['g_out_b'])
    x = x + jnp.concatenate([out_a, out_b], axis=-1) @ p['w_out']
    f, conv_new = conv_ffn(x, conv_prev, p['g_ffn'], p['w_up'], p['w_conv'], p['b_conv'], p['w_down'])
    return x + f, c_kv, k_rope, v, conv_new


def setup_inputs(seed: int = 0) -> dict:
    key = jax.random.key(seed)
    ks = jax.random.split(key, 32)
    f32 = jnp.float32
    n_pages = PAST_LEN // PAGE_SIZE
    n_used = DEC_BATCH * n_pages
    n_pool = n_used + n_used // 4

    def nrm(k, shape, scale):
        return jax.random.normal(k, shape, f32) * scale

    def gain(k, shape):
        return 1.0 + 0.01 * jax.random.normal(k, shape, f32)

    page_table = jax.random.permutation(ks[0], n_pool)[:n_used].reshape(DEC_BATCH, n_pages).astype(jnp.int32)
    return {
        'x_prompt': nrm(ks[1], (BATCH, SEQ, D_MODEL), 1.0),
        'x_sample': nrm(ks[2], (DEC_BATCH, DEC_SEQ, D_MODEL), 1.0),
        'cache_kv_latent': nrm(ks[3], (DEPTH, n_pool, PAGE_SIZE, KV_RANK), 1.0),
        'cache_k_rope': nrm(ks[4], (DEPTH, n_pool, PAGE_SIZE, QK_ROPE_DIM), 1.0),
        'state_ffn_conv': nrm(ks[5], (DEPTH, DEC_BATCH, CONV_W - 1, 2 * D_FF), 1.0),
        'page_table': page_table,
        'g_mix': gain(ks[6], (DEPTH, D_MODEL)),
        'w_in': nrm(ks[7], (DEPTH, D_MODEL, D_IN), D_MODEL ** -0.5),
        'g_sgu': gain(ks[8], (DEPTH, D_A)),
        'w_spatial': nrm(ks[9], (DEPTH, N_HEADS_A, CHUNK, CHUNK), CHUNK ** -0.5),
        'b_spatial': gain(ks[10], (DEPTH, N_HEADS_A, CHUNK)),
        'g_q': gain(ks[11], (DEPTH, Q_RANK)),
        'w_uq': nrm(ks[12], (DEPTH, Q_RANK, N_HEADS_B, QK_NOPE_DIM + QK_ROPE_DIM), Q_RANK ** -0.5),
        'g_kv': gain(ks[13], (DEPTH, KV_RANK)),
        'w_uk': nrm(ks[14], (DEPTH, KV_RANK, N_HEADS_B, QK_NOPE_DIM), KV_RANK ** -0.5),
        'w_uv': nrm(ks[15], (DEPTH, KV_RANK, N_HEADS_B, V_HEAD_DIM), KV_RANK ** -0.5),
        'g_out_a': gain(ks[16], (DEPTH, D_A)),
        'g_out_b': gain(ks[17], (DEPTH, D_B)),
        'w_out': nrm(ks[18], (DEPTH, D_MIX, D_MODEL), D_MIX ** -0.5),
        'g_ffn': gain(ks[19], (DEPTH, D_MODEL)),
        'w_up': nrm(ks[20], (DEPTH, D_MODEL, 2 * D_FF), D_MODEL ** -0.5),
        'w_conv': nrm(ks[21], (DEPTH, CONV_W, 2 * D_FF), CONV_W ** -0.5),
        'b_conv': nrm(ks[22], (DEPTH, 2 * D_FF), 0.02),
        'w_down': nrm(ks[23], (DEPTH, D_FF, D_MODEL), D_FF ** -0.5),
        'g_final': gain(ks[24], (D_MODEL,)),
    }


def reference(x_prompt, x_sample, cache_kv_latent, cache_k_rope, state_ffn_conv, page_table,
              g_mix, w_in, g_sgu, w_spatial, b_spatial, g_q, w_uq, g_kv, w_uk, w_uv,
              g_out_a, g_out_b, w_out, g_ffn, w_up, w_conv, b_conv, w_down, g_final):
    B, S = x_prompt.shape[:2]
    DB, T = x_sample.shape[:2]
    past_len = page_table.shape[1] * cache_kv_latent.shape[2]
    pos_p = jnp.arange(S, dtype=jnp.int32)
    pos_s = past_len + jnp.arange(T, dtype=jnp.int32)
    yp, ys = x_prompt, x_sample
    p_lat, p_kr, p_conv, s_lat, s_kr, s_v, s_conv = [], [], [], [], [], [], []
    for l in range(DEPTH):
        p = {'g_mix': g_mix[l], 'w_in': w_in[l], 'g_sgu': g_sgu[l], 'w_spatial': w_spatial[l],
             'b_spatial': b_spatial[l], 'g_q': g_q[l], 'w_uq': w_uq[l], 'g_kv': g_kv[l],
             'w_uk': w_uk[l], 'w_uv': w_uv[l], 'g_out_a': g_out_a[l], 'g_out_b': g_out_b[l],
             'w_out': w_out[l], 'g_ffn': g_ffn[l], 'w_up': w_up[l], 'w_conv': w_conv[l],
             'b_conv': b_conv[l], 'w_down': w_down[l]}
        attend_p = lambda qn, qr, c, kr, p=p: mla_prompt(qn, qr, c, kr, p['w_uk'], p['w_uv'])
        conv0 = jnp.zeros((B, CONV_W - 1, 2 * D_FF), yp.dtype)
        yp, c_p, kr_p, _, conv_p = layer(yp, pos_p, conv0, CHUNK, attend_p, p)
        c_past = cache_kv_latent[l, page_table].reshape(DB, -1, KV_RANK)
        kr_past = cache_k_rope[l, page_table].reshape(DB, -1, QK_ROPE_DIM)
        attend_s = lambda qn, qr, c, kr, p=p, c_past=c_past, kr_past=kr_past: mla_sample(
            qn, qr, c, kr, c_past, kr_past, p['w_uk'], p['w_uv'])
        ys, c_s, kr_s, v_s, conv_s = layer(ys, pos_s, state_ffn_conv[l], T, attend_s, p)
        p_lat.append(c_p); p_kr.append(kr_p); p_conv.append(conv_p)
        s_lat.append(c_s); s_kr.append(kr_s); s_v.append(v_s); s_conv.append(conv_s)
    y_prompt = rmsnorm(yp, g_final)
    y_sample = rmsnorm(ys, g_final)
    return (y_prompt, y_sample, jnp.stack(p_lat), jnp.stack(p_kr), jnp.stack(p_conv),
            jnp.stack(s_lat), jnp.stack(s_kr), jnp.stack(s_v), jnp.stack(s_conv))
```

```python
import numpy as np
from contextlib import ExitStack
import concourse.bass as bass
import concourse.mybir as mybir
from concourse.bass_utils import run_bass_kernel_spmd

F32 = mybir.dt.float32
BF16 = mybir.dt.bfloat16
I32 = mybir.dt.int32
AF = mybir.ActivationFunctionType
ALU = mybir.AluOpType
AX = mybir.AxisListType

D = 1024
KD = 8
D_A = 512
QR = 384
KVR = 256
ROPE = 32
HALF = 16
D_IN = 1696
D_FF = 2816
NFT = 22
NH = 8
EPS = 1e-6
ATTN_SCALE = 96.0 ** -0.5
NCORES = 8
NS = 128
TB = 256
NSUB = TB // 128


class Cfg:
    def __init__(self, nseq=2, seq=2048, npl=2560, npages=128, ncores=8, do_sample=True, debug=False):
        self.debug = debug
        self.reserve = 512
        self.stages = ("s1", "E", "loop", "cc", "prompt", "s3")
        self.estop = 99
        self.nseq = nseq
        self.seq = seq
        self.npl = npl
        self.npages = npages
        self.ncores = ncores
        self.do_sample = do_sample


class Prog:
    ENGS = ("pe", "act", "dve", "pool", "sp")

    def __init__(self):
        self.ops = []
        self.lastw = {}
        self.readers = {}
        self.lastdma = {}
        self.out_keys = set()
        self.bar = None
        self.bar_done = set()

    def barrier(self):
        last = {}
        for i, o in enumerate(self.ops):
            if not o["dma"]:
                last[o["eng"]] = i
        self.bar = set(last.values()) | set(self.lastdma.values())
        self.bar_done = set()

    def add(self, eng, fn, reads=(), writes=(), dma=False, semkey=None, is_out=False, inc=16):
        i = len(self.ops)
        psr = [k for k in reads if k.startswith("ps") and k not in writes]
        if psr:
            writes = list(writes) + psr
        deps = {}
        for k in reads:
            if k in self.lastw:
                deps[self.lastw[k]] = True
        for k in writes:
            if k in self.lastw:
                deps.setdefault(self.lastw[k], False)
            for r in self.readers.get(k, ()):
                deps.setdefault(r, False)
        if dma:
            assert semkey is not None
            if semkey in self.lastdma:
                deps.setdefault(self.lastdma[semkey], False)
            self.lastdma[semkey] = i
            if is_out:
                self.out_keys.add(semkey)
        for k in writes:
            self.lastw[k] = i
            self.readers[k] = []
        for k in reads:
            if k not in writes:
                self.readers.setdefault(k, []).append(i)
        if self.bar is not None and eng not in self.bar_done:
            self.bar_done.add(eng)
            for d in self.bar:
                deps[d] = True
        deps.pop(i, None)
        self.ops.append(dict(eng=eng, fn=fn, deps=deps, dma=dma, semkey=semkey, inc=inc))
        return i

    @staticmethod
    def _edge(p, o, raw):
        if p["dma"] or o["dma"] or p["eng"] != o["eng"]:
            return True
        return raw and p["eng"] != "pe"

    def emit(self, nc, es):
        ops = self.ops
        n = len(ops)
        need = [False] * n
        for i, o in enumerate(ops):
            for d, raw in o["deps"].items():
                if self._edge(ops[d], o, raw):
                    need[d] = True
        cnt = {e: 0 for e in self.ENGS}
        val = [0] * n
        dcnt = {}
        for i, o in enumerate(ops):
            if o["dma"]:
                k = o["semkey"]
                dcnt[k] = dcnt.get(k, 0) + o["inc"]
                val[i] = dcnt[k]
            elif need[i]:
                cnt[o["eng"]] += 1
                val[i] = cnt[o["eng"]]
        esem = {e: es.enter_context(nc.semaphore("s_" + e)) for e in self.ENGS}
        dsem = {}
        for k in dcnt:
            dsem[k] = es.enter_context(nc.semaphore("d%d" % len(dsem)))
        self.n_sems = len(esem) + len(dsem)
        by_eng = {e: [i for i, o in enumerate(ops) if o["eng"] == e] for e in self.ENGS}
        out_final = {k: dcnt[k] for k in self.out_keys}

        def run(ename, eng):
            waited = {}
            for i in by_eng[ename]:
                o = ops[i]
                for d in sorted(o["deps"]):
                    p = ops[d]
                    if not self._edge(p, o, o["deps"][d]):
                        continue
                    if p["dma"]:
                        sem, v, sk = dsem[p["semkey"]], val[d], ("d", p["semkey"])
                    else:
                        sem, v, sk = esem[p["eng"]], val[d], ("e", p["eng"])
                    if waited.get(sk, 0) >= v:
                        continue
                    waited[sk] = v
                    eng.wait_ge(sem, v)
                inst = o["fn"](eng)
                if o["dma"]:
                    inst.then_inc(dsem[o["semkey"]], o["inc"])
                elif need[i]:
                    inst.then_inc(esem[ename], 1)
            if ename == "sp":
                for k, v in out_final.items():
                    eng.wait_ge(dsem[k], v)

        block = es.enter_context(nc.Block())

        @block.sync
        def _(e):
            run("sp", e)

        @block.tensor
        def _(e):
            run("pe", e)

        @block.scalar
        def _(e):
            run("act", e)

        @block.vector
        def _(e):
            run("dve", e)

        @block.gpsimd
        def _(e):
            run("pool", e)


def build(cfg):
    nc = bass.Bass("TRN2", target_bir_lowering=False)
    P = Prog()
    es = ExitStack()
    NSEQ, SEQ, NPL, NPAGES = cfg.nseq, cfg.seq, cfg.npl, cfg.npages
    NBLK = SEQ // TB
    NKT = SEQ // 128
    NGRP = NPL // 128

    def din(name, shape, dt=F32):
        return nc.dram_tensor(name, list(shape), dt, kind="ExternalInput").ap()

    def dout(name, shape, dt=F32):
        return nc.dram_tensor(name, list(shape), dt, kind="ExternalOutput").ap()

    def dint(name, shape, dt=F32):
        return nc.dram_tensor(name, list(shape), dt, kind="Internal").ap()

    xp = din("xp", [NSEQ, SEQ, D])
    g_mix = din("g_mix", [D]); w_in = din("w_in", [D, D_IN])
    g_sgu = din("g_sgu", [D_A]); w_sp = din("w_sp", [NH, 128, 128]); b_sp = din("b_sp", [NH, 128])
    g_q = din("g_q", [QR]); w_uq = din("w_uq", [QR, NH * 96])
    g_kv = din("g_kv", [KVR]); w_uk = din("w_uk", [KVR, NH * 64]); w_uv = din("w_uv", [KVR, NH * 64])
    g_oa = din("g_oa", [D_A]); g_ob = din("g_ob", [D_A]); w_out = din("w_out", [D, D])
    g_ffn = din("g_ffn", [D]); w_up = din("w_up", [D, 2 * D_FF]); w_conv = din("w_conv", [3, 2 * D_FF])
    b_conv = din("b_conv", [2 * D_FF]); w_down = din("w_down", [D_FF, D]); g_fin = din("g_fin", [D])
    c_ident = din("c_ident", [128, 128])
    c_mask = din("c_mask", [128, 128])
    c_cosp = din("c_cosp", [SEQ, HALF]); c_sinp = din("c_sinp", [SEQ, HALF])
    yp = dout("yp", [NSEQ, SEQ, D])
    o_plat = dout("o_plat", [NSEQ, SEQ, KVR])
    o_pkr = dout("o_pkr", [NSEQ, SEQ, ROPE])
    o_pconv = dout("o_pconv", [NSEQ, 2, 2 * D_FF])
    if cfg.debug:
        dbg_cs = dout("dbg_cs", [SEQ, 2 * HALF]); dbg_krp = dout("dbg_krp", [SEQ, ROPE])
    if cfg.do_sample:
        xs = din("xs", [NS, D])
        lat = din("lat", [NPL, 128, KVR]); krc = din("krc", [NPL, 128, ROPE])
        st = din("st", [NS, 2, 2 * D_FF])
        ptab = din("ptab", [NS, NPAGES], I32)
        c_coss = din("c_coss", [NS, HALF]); c_sins = din("c_sins", [NS, HALF])
        c_pgbase = din("c_pgbase", [128, 1])
        c_iota = din("c_iota", [128, 128]); c_glo = din("c_glo", [128, NGRP])
        accd_in = dint("accd_in", [128, NH * 257]); accd_out = dint("accd_out", [128, NH * 257])
        ys = dout("ys", [NS, D])
        if cfg.debug:
            dbg_acc = dout("dbg_acc", [128, NH * 257]); dbg_E = dout("dbg_E", [128, NGRP * 128])
            dbg_q = dout("dbg_q", [128, NH * 288]); dbg_en = dout("dbg_en", [128, NH])
        o_slat = dout("o_slat", [NS, KVR]); o_skr = dout("o_skr", [NS, ROPE])
        o_sv = dout("o_sv", [NS, D_A]); o_sconv = dout("o_sconv", [NS, 2, 2 * D_FF])
    wup_s = dint("wup_s", [NFT, 128, 2 * KD * 128], BF16)
    wdn_s = dint("wdn_s", [NFT // 2, 128, 2 * D], BF16)

    resid = [0]

    def sbr(name, shape, dt=F32):
        n = 1
        for d_ in shape[1:]:
            n *= d_
        resid[0] += (n * (2 if dt == BF16 else 4) + 63) // 64 * 64
        return es.enter_context(nc.sbuf_tensor(name, list(shape), dt))

    arena = {"t": None, "off": 0, "size": 0}

    def sb(name, shape, dt=F32):
        n = 1
        for d_ in shape[1:]:
            n *= d_
        words = n if dt != BF16 else (n + 1) // 2
        words = (words + 7) // 8 * 8
        off = arena["off"]
        assert off + words <= arena["size"], ("arena overflow", name, off, words, arena["size"])
        arena["off"] = off + words
        ap = arena["t"][0:shape[0], off:off + words]
        if dt != F32:
            ap = ap.bitcast(dt)
        ap = ap[:, 0:n]
        if len(shape) == 3:
            ap = ap.rearrange("p (a b) -> p a b", a=shape[1])
        elif len(shape) == 4:
            ap = ap.rearrange("p (a b c) -> p a b c", a=shape[1], b=shape[2])
        return ap

    ps = es.enter_context(nc.psum_tensor("ps", [128, 8, 512], F32))
    psctr = [0]
    psn = [6]

    def bank(n=1):
        b = psctr[0]
        if n == 2 and b % 2:
            b += 1
        if b + n > psn[0]:
            b = 0
        psctr[0] = (b + n) % psn[0]
        return b

    def pk(b, n=1):
        return ["ps%d" % (b + i) for i in range(n)]

    def dma(out, in_, reads, writes, semkey, is_out=False, eng="sp", slow=False):
        kw = dict(allow_slow_non_contiguous=True) if slow else {}
        P.add(eng, lambda e: e.dma_start(out=out, in_=in_, **kw), reads, writes, dma=True,
              semkey=semkey, is_out=is_out)

    def mm(out, lhsT, rhs, start, stop, reads, writes, skip=False):
        P.add("pe", lambda e: e.matmul(out, lhsT, rhs, start=start, stop=stop, skip_group_check=skip),
              reads, writes)

    def tr(out, in_, ident, reads, writes):
        P.add("pe", lambda e: e.transpose(out, in_, ident), reads, writes)

    def act(out, in_, func, reads, writes, scale=None, bias=None, accum=None):
        kw = {}
        if scale is not None:
            kw["scale"] = scale
        if bias is not None:
            kw["bias"] = bias
        if accum is not None:
            kw["accum_out"] = accum
        P.add("act", lambda e: e.activation(out, in_, func, **kw), reads, writes)

    def tt(eng, out, a, b, op, reads, writes):
        P.add(eng, lambda e: e.tensor_tensor(out, a, b, op), reads, writes)

    def ts(eng, out, a, s1, s2, op0, op1, reads, writes):
        if op1 is None:
            P.add(eng, lambda e: e.tensor_scalar(out, a, s1, None, op0), reads, writes)
        else:
            P.add(eng, lambda e: e.tensor_scalar(out, a, s1, s2, op0, op1), reads, writes)

    def stt(out, a, s, b, op0, op1, reads, writes):
        P.add("dve", lambda e: e.scalar_tensor_tensor(out, a, s, b, op0, op1), reads, writes)

    def cp(eng, out, in_, reads, writes):
        if eng == "act":
            P.add("act", lambda e: e.copy(out, in_), reads, writes)
        else:
            P.add(eng, lambda e: e.tensor_copy(out, in_), reads, writes)

    def recip(out, in_, reads, writes):
        P.add("dve", lambda e: e.reciprocal(out, in_), reads, writes)

    def memset(eng, ap, v, writes):
        P.add(eng, lambda e: e.memset(ap, v), (), writes)

    ident_f = sbr("ident_f", [128, 128]); ident_b = sbr("ident_b", [128, 128], BF16)
    maskT = sbr("maskT", [128, 128], BF16)
    eps_t = sbr("eps_t", [128, 1])
    w_in_b = sbr("w_in_b", [128, KD, D_IN], BF16)
    w_uq_b = sbr("w_uq_b", [128, 3, NH * 96], BF16)
    w_uk_b = sbr("w_uk_b", [128, 2, 512], BF16)
    w_uv_b = sbr("w_uv_b", [128, 2, 512], BF16)
    w_out_b = sbr("w_out_b", [128, KD, D], BF16)
    wsT = sbr("wsT", [128, NH, 128], BF16)
    bspT = sbr("bspT", [128, NH])
    Gsgu = sbr("Gsgu", [128, D_A]); Gkv = sbr("Gkv", [128, KVR]); Gfin = sbr("Gfin", [128, D])
    gk = sbr("gk", [128, 4, KD])
    wc = sbr("wc", [128, 3, 2 * NFT]); bc = sbr("bc", [128, 2 * NFT])
    if cfg.do_sample:
        w_ukT = sbr("w_ukT", [64, NH, KVR], BF16)
        W0 = sbr("W0", [128, NH]); B0 = sbr("B0", [128, NH])
        s_mix = sbr("s_mix", [128, D_A], BF16)
        acc_sb = sbr("acc_sb", [128, NH, 257])
        e_new = sbr("e_new", [128, NH])
        cnew_f = sbr("cnew_f", [128, KVR + ROPE])
        pgbase = sbr("pgbase", [128, 1]); iota_r = sbr("iota_r", [128, 128]); glo = sbr("glo", [128, NGRP])
        ones_b = sbr("ones_b", [128, 1], BF16)
    SBUF_BYTES = 229344
    arena["size"] = (SBUF_BYTES - 16384 - resid[0] - cfg.reserve) // 4 // 8 * 8
    arena["t"] = es.enter_context(nc.sbuf_tensor("arena", [128, arena["size"]], F32))
    stage = [sb("stage%d" % i, [128, 2048]) for i in range(2)]
    stg_b = [sb("stgb%d" % i, [128, 2048], BF16) for i in range(2)]

    dma(ident_f[:], c_ident, [], ["ident_f"], "ident_f")
    cp("dve", ident_b[:], ident_f[:], ["ident_f"], ["ident_b"])
    dma(stage[0][:, 0:128], c_mask, [], ["stage0"], "stage0")
    cp("dve", maskT[:], stage[0][:, 0:128], ["stage0"], ["maskT"])
    memset("dve", eps_t[:], EPS, ["eps_t"])
    dma(gk[:, 0, :], g_mix.rearrange("(k p) -> p k", p=128), [], ["gk"], "gk", slow=True)
    dma(gk[:, 1, 0:3], g_q.rearrange("(k p) -> p k", p=128), [], ["gk"], "gk", slow=True)
    dma(gk[:, 2, 0:4], g_oa.rearrange("(k p) -> p k", p=128), [], ["gk"], "gk", slow=True)
    dma(gk[:, 2, 4:8], g_ob.rearrange("(k p) -> p k", p=128), [], ["gk"], "gk", slow=True)
    dma(gk[:, 3, :], g_ffn.rearrange("(k p) -> p k", p=128), [], ["gk"], "gk", slow=True)
    dma(Gsgu[:], g_sgu.partition_broadcast(128), [], ["Gsgu"], "Gsgu")
    dma(Gkv[:], g_kv.partition_broadcast(128), [], ["Gkv"], "Gkv")
    dma(Gfin[:], g_fin.partition_broadcast(128), [], ["Gfin"], "Gfin")
    for k in range(3):
        dma(wc[:, k, :], w_conv[k].rearrange("(f p) -> p f", p=128), [], ["wc"], "wc", slow=True)
    dma(bc[:], b_conv.rearrange("(f p) -> p f", p=128), [], ["bc"], "bc", slow=True)
    dma(bspT[:], b_sp.rearrange("h t -> t h"), [], ["bspT"], "bspT", slow=True)

    wrr = [0]
    ceng = ["dve", "pool", "act"]

    def prep(dst_fn, src_ap, ncols, gain_ap, dst_keys, to_dram=None, store_view=None):
        i = wrr[0] % 2
        wrr[0] += 1
        sk = "stage%d" % i
        dma(stage[i][:, 0:ncols], src_ap, [], [sk], sk)
        eng = ceng[wrr[0] % 3]
        if to_dram is None:
            dst = dst_fn
            wk = dst_keys
        else:
            dst = stg_b[i][:, 0:ncols]
            wk = ["stgb%d" % i]
        if gain_ap is None:
            cp(eng, dst, stage[i][:, 0:ncols], [sk], wk)
        elif eng == "act":
            act(dst, stage[i][:, 0:ncols], AF.Copy, [sk, "gk"], wk, scale=gain_ap)
        else:
            ts(eng, dst, stage[i][:, 0:ncols], gain_ap, None, ALU.mult, None, [sk, "gk"], wk)
        if to_dram is not None:
            srcv = stg_b[i][:, 0:ncols]
            if store_view is not None:
                srcv = store_view(srcv)
            dma(to_dram, srcv, wk, dst_keys, "stgb%d" % i)

    for kt in range(KD):
        prep(w_in_b[:, kt, :], w_in[kt * 128:(kt + 1) * 128, :], D_IN, gk[:, 0, kt:kt + 1], ["w_in_b"])
    for kt in range(3):
        prep(w_uq_b[:, kt, :], w_uq[kt * 128:(kt + 1) * 128, :], NH * 96, gk[:, 1, kt:kt + 1], ["w_uq_b"])
    for kt in range(2):
        prep(w_uk_b[:, kt, :], w_uk[kt * 128:(kt + 1) * 128, :], 512, None, ["w_uk_b"])
        prep(w_uv_b[:, kt, :], w_uv[kt * 128:(kt + 1) * 128, :], 512, None, ["w_uv_b"])
    for kt in range(KD):
        prep(w_out_b[:, kt, :], w_out[kt * 128:(kt + 1) * 128, :], D, gk[:, 2, kt:kt + 1], ["w_out_b"])
    wup_v = wup_s.rearrange("ft p (gv kt f) -> ft p gv kt f", gv=2, kt=KD)
    for kt in range(KD):
        for gv in range(2):
            for half in range(2):
                c0 = gv * D_FF + half * 1408
                f0 = half * 11
                src = w_up[kt * 128:(kt + 1) * 128, c0:c0 + 1408]
                dstd = wup_v[f0:f0 + 11, :, gv, kt, :].rearrange("ft p f -> p ft f")
                prep(None, src, 1408, gk[:, 3, kt:kt + 1], ["wup_s"], to_dram=dstd,
                     store_view=lambda a: a.rearrange("p (ft f) -> p ft f", ft=11))
    wdn_v = wdn_s.rearrange("g p (two n) -> g p two n", two=2)
    for kt in range(NFT):
        prep(None, w_down[kt * 128:(kt + 1) * 128, :], D, None, ["wdn_s"],
             to_dram=wdn_v[kt // 2, :, kt % 2, :])
    for h in range(NH):
        i = wrr[0] % 2
        wrr[0] += 1
        sk = "stage%d" % i
        dma(stage[i][:, 0:128], w_sp[h], [], [sk], sk)
        b = bank()
        tr(ps[:, b, 0:128], stage[i][:, 0:128], ident_f[:], [sk, "ident_f"], pk(b))
        tt("dve", wsT[:, h, :], ps[:, b, 0:128], maskT[:], ALU.mult, pk(b) + ["maskT"], ["wsT"])

    if cfg.do_sample:
        for h in range(NH):
            if h % 4 == 0:
                bw = bank()
                pbw = ps[:, bw, :].bitcast(BF16)
            for kt in range(2):
                col = ((h % 4) * 2 + kt) * 128
                tr(pbw[0:64, col:col + 128], w_uk_b[:, kt, h * 64:(h + 1) * 64], ident_b[:],
                   ["w_uk_b", "ident_b"], pk(bw))
            if h % 4 == 3:
                cp("act", w_ukT[:, h - 3:h + 1, :], pbw[0:64, :].rearrange("p (h r) -> p h r", h=4), pk(bw),
                   ["w_ukT"])
        dma(W0[:], w_sp[:, 0, 0].partition_broadcast(128), [], ["W0"], "W0", slow=True)
        dma(B0[:], b_sp[:, 0].partition_broadcast(128), [], ["B0"], "B0", slow=True)
        dma(pgbase[:], c_pgbase, [], ["pgbase"], "pgbase")
        dma(iota_r[:], c_iota, [], ["iota_r"], "iota_r")
        dma(glo[:], c_glo, [], ["glo"], "glo")
        memset("dve", ones_b[:], 1.0, ["ones_b"])

    class NS_:
        pass
    B = NS_()

    def alloc_front():
        B.x2 = sb("x2", [128, NSUB, D])
        B.xT = sb("xT", [128, KD, 128], BF16)
        B.junk = sb("junk", [128, D], BF16)
        B.u_sb = sb("u_sb", [128, D_A]); B.vg = sb("vg", [128, D_A]); B.v_bf = sb("v_bf", [128, D_A], BF16)
        B.cq_bf = sb("cq_bf", [128, QR], BF16); B.cqT = sb("cqT", [128, 3, 128], BF16)
        B.ckv = sb("ckv", [128, KVR]); B.ckv_bf = sb("ckv_bf", [128, KVR], BF16)
        B.ckvT = sb("ckvT", [128, 2, 128], BF16)
        B.krp = sb("krp", [128, ROPE]); B.krr = sb("krr", [128, ROPE])
        B.qs = sb("qs", [128, NH, 96], BF16); B.qf = sb("qf", [128, NH, 96])
        B.ks = sb("ks", [128, NH, 96], BF16)
        B.cs = sb("cs", [128, 2, HALF])
        B.sm = sb("sm", [128, 16])
        B.rtmp = sb("rtmp", [128, NH, 2, HALF]); B.rtk = sb("rtk", [128, 2, HALF])
        B.gate_t = sb("gate_t", [128, D_A])

    def alloc_main():
        alloc_front()
        B.kT = sb("kT", [96, NH, SEQ], BF16)
        B.Vaug = sb("Vaug", [128, NKT, NH, 65], BF16)
        B.qT = sb("qT", [96, NH, TB], BF16)
        B.attn_o = sb("attn_o", [128, NSUB, D_A])
        B.mixbf = sb("mixbf", [128, NSUB, D], BF16)
        B.mT = sb("mT", [128, KD, 128], BF16)
        B.h2T = sb("h2T", [128, KD, TB + 2], BF16)
        B.actT = sb("actT", [128, NFT, TB], BF16)
        B.ptile = [sb("pt%d" % i, [128, TB], BF16) for i in range(3)]
        B.o_sb = sb("o_sb", [128, NSUB, 65]); B.rden = sb("rden", [128, NSUB])
        B.wu = [sb("wu%d" % i, [128, 2, KD, 128], BF16) for i in range(2)]
        B.wd = [sb("wd%d" % i, [128, 2, D], BF16) for i in range(2)]
        B.upg = sb("upg", [128, TB + 2]); B.upv = sb("upv", [128, TB + 2])
        B.cg = sb("cg", [128, TB]); B.cv = sb("cv", [128, TB])
        B.halo = sb("halo", [128, 2 * NFT, 2])
        if cfg.do_sample:
            B.stf = B.qf[:].rearrange("p h d -> p (h d)")[:, 0:512].rearrange("p (a b c) -> p a b c", a=2, b=2)
            B.upn = B.gate_t[:, 0:256].rearrange("p (g f) -> p g f", g=2)

    rr = dict(xt=0, pt=0, wu=0, wd=0, yt=0, blk=0)

    def rms_scale(dst, ss_ap, n, reads, writes):
        act(dst, ss_ap, AF.Sqrt, reads + ["eps_t"], writes, scale=1.0 / n, bias=eps_t[:])
        recip(dst, dst, writes, writes)

    def front(x_src, cos_src, sin_src, is_sample, sub=0, tok0=0, lat_out=None, kr_out=None, v_out=None):
        xk = "x2_%d" % sub
        xtile = B.x2[:, sub, :]
        dma(xtile, x_src, [], [xk], xk)
        dma(B.cs[:, 0, :], cos_src, [], ["cs"], "cs")
        dma(B.cs[:, 1, :], sin_src, [], ["cs"], "cs")
        act(B.junk[:], xtile, AF.Square, [xk], ["junk", "sm0"], accum=B.sm[:, 0:1])
        rms_scale(B.sm[:, 0:1], B.sm[:, 0:1], D, ["sm0"], ["sm0"])
        r = B.sm[:, 0:1]
        b = bank(2)
        for kt in range(KD):
            tr(ps[:, b + kt // 4, (kt % 4) * 128:(kt % 4 + 1) * 128], xtile[:, kt * 128:(kt + 1) * 128],
               ident_f[:], [xk, "ident_f"], pk(b + kt // 4))
        cp("dve", B.xT[:, 0:4, :], ps[:, b, :].rearrange("p (k t) -> p k t", k=4), pk(b), ["xTa"])
        cp("act", B.xT[:, 4:8, :], ps[:, b + 1, :].rearrange("p (k t) -> p k t", k=4), pk(b + 1), ["xTb"])
        bu, bv, bq, bk = bank(), bank(), bank(), bank()
        for (bb, c0, n) in ((bu, 0, 512), (bv, 512, 512), (bq, 1024, QR), (bk, 1024 + QR, KVR + ROPE)):
            for kt in range(KD):
                mm(ps[:, bb, 0:n], B.xT[:, kt, :], w_in_b[:, kt, c0:c0 + n], kt == 0, kt == KD - 1,
                   ["xTa", "xTb", "w_in_b"], pk(bb))
        act(B.u_sb[:], ps[:, bu, :], AF.Gelu, pk(bu) + ["sm0"], ["u_sb"], scale=r)
        act(B.vg[:], ps[:, bv, :], AF.Gelu, pk(bv) + ["sm0"], ["vg"], scale=r)
        act(B.junk[:, 0:D_A], B.vg[:], AF.Square, ["vg"], ["junk", "sm1"], accum=B.sm[:, 1:2])
        rms_scale(B.sm[:, 1:2], B.sm[:, 1:2], D_A, ["sm1"], ["sm1"])
        stt(B.vg[:], B.vg[:], B.sm[:, 1:2], Gsgu[:], ALU.mult, ALU.mult, ["vg", "sm1", "Gsgu"], ["vg"])
        if v_out is not None:
            dma(v_out, B.vg[:], ["vg"], [], "vg", is_out=True)
        act(B.junk[:, 0:QR], ps[:, bq, 0:QR], AF.Square, pk(bq) + ["sm0"], ["junk", "sm2"], scale=r,
            accum=B.sm[:, 2:3])
        rms_scale(B.sm[:, 2:3], B.sm[:, 2:3], QR, ["sm2"], ["sm2"])
        ts("dve", B.cq_bf[:], ps[:, bq, 0:QR], r, B.sm[:, 2:3], ALU.mult, ALU.mult, pk(bq) + ["sm0", "sm2"],
           ["cq_bf"])
        act(B.junk[:, 0:KVR], ps[:, bk, 0:KVR], AF.Square, pk(bk) + ["sm0"], ["junk", "sm3"], scale=r,
            accum=B.sm[:, 3:4])
        rms_scale(B.sm[:, 3:4], B.sm[:, 3:4], KVR, ["sm3"], ["sm3"])
        ts("dve", B.ckv[:], ps[:, bk, 0:KVR], r, B.sm[:, 3:4], ALU.mult, ALU.mult, pk(bk) + ["sm0", "sm3"],
           ["ckv"])
        tt("dve", B.ckv[:], B.ckv[:], Gkv[:], ALU.mult, ["ckv", "Gkv"], ["ckv"])
        cp("pool", B.ckv_bf[:], B.ckv[:], ["ckv"], ["ckv_bf"])
        if lat_out is not None:
            dma(lat_out, B.ckv[:], ["ckv"], [], "ckv", is_out=True)
        ts("dve", B.krp[:], ps[:, bk, KVR:KVR + ROPE], r, None, ALU.mult, None, pk(bk) + ["sm0"], ["krp"])
        c_, s_ = B.cs[:, 0, :], B.cs[:, 1, :]
        if cfg.debug and not is_sample:
            dma(dbg_cs[tok0:tok0 + 128, :], B.cs[:].rearrange("p a b -> p (a b)"), ["cs"], [], "dbg1", is_out=True)
            dma(dbg_krp[tok0:tok0 + 128, :], B.krp[:], ["krp"], [], "dbg2", is_out=True)
        x1, x2_ = B.krp[:, 0:HALF], B.krp[:, HALF:ROPE]
        tt("dve", B.rtk[:, 0, :], x1, c_, ALU.mult, ["krp", "cs"], ["rtk"])
        tt("dve", B.rtk[:, 1, :], x2_, s_, ALU.mult, ["krp", "cs"], ["rtk"])
        tt("dve", B.krr[:, 0:HALF], B.rtk[:, 0, :], B.rtk[:, 1, :], ALU.subtract, ["rtk"], ["krr"])
        tt("dve", B.rtk[:, 0, :], x1, s_, ALU.mult, ["krp", "cs"], ["rtk"])
        tt("dve", B.rtk[:, 1, :], x2_, c_, ALU.mult, ["krp", "cs"], ["rtk"])
        tt("dve", B.krr[:, HALF:ROPE], B.rtk[:, 0, :], B.rtk[:, 1, :], ALU.add, ["rtk"], ["krr"])
        if kr_out is not None:
            dma(kr_out, B.krr[:], ["krr"], [], "krr", is_out=True)
        b = bank()
        pb = ps[:, b, :].bitcast(BF16)
        for kt in range(3):
            tr(pb[:, kt * 128:(kt + 1) * 128], B.cq_bf[:, kt * 128:(kt + 1) * 128], ident_b[:],
               ["cq_bf", "ident_b"], pk(b))
        cp("act", B.cqT[:], pb[:, 0:384].rearrange("p (k t) -> p k t", k=3), pk(b), ["cqT"])
        bq1, bq2 = bank(), bank()
        for (bb, c0) in ((bq1, 0), (bq2, 384)):
            for kt in range(3):
                mm(ps[:, bb, 0:384], B.cqT[:, kt, :], w_uq_b[:, kt, c0:c0 + 384], kt == 0, kt == 2,
                   ["cqT", "w_uq_b"], pk(bb))
        cp("act", B.qf[:, 0:4, :], ps[:, bq1, 0:384].rearrange("p (h d) -> p h d", h=4), pk(bq1), ["qf"])
        cp("dve", B.qf[:, 4:8, :], ps[:, bq2, 0:384].rearrange("p (h d) -> p h d", h=4), pk(bq2), ["qf"])
        cp("pool", B.qs[:, :, 0:64], B.qf[:, :, 0:64], ["qf"], ["qs"])
        cb = B.cs[:, 0:1, :].to_broadcast([128, NH, HALF])
        sbb = B.cs[:, 1:2, :].to_broadcast([128, NH, HALF])
        q1, q2 = B.qf[:, :, 64:80], B.qf[:, :, 80:96]
        tt("dve", B.rtmp[:, :, 0, :], q1, cb, ALU.mult, ["qf", "cs"], ["rtmp"])
        tt("pool", B.rtmp[:, :, 1, :], q2, sbb, ALU.mult, ["qf", "cs"], ["rtmp1"])
        tt("dve", B.qs[:, :, 64:80], B.rtmp[:, :, 0, :], B.rtmp[:, :, 1, :], ALU.subtract, ["rtmp", "rtmp1"], ["qs"])
        tt("dve", B.rtmp[:, :, 0, :], q1, sbb, ALU.mult, ["qf", "cs"], ["rtmp"])
        tt("pool", B.rtmp[:, :, 1, :], q2, cb, ALU.mult, ["qf", "cs"], ["rtmp1"])
        tt("dve", B.qs[:, :, 80:96], B.rtmp[:, :, 0, :], B.rtmp[:, :, 1, :], ALU.add, ["rtmp", "rtmp1"], ["qs"])
        b = bank()
        pb = ps[:, b, :].bitcast(BF16)
        for kt in range(2):
            tr(pb[:, kt * 128:(kt + 1) * 128], B.ckv_bf[:, kt * 128:(kt + 1) * 128], ident_b[:],
               ["ckv_bf", "ident_b"], pk(b))
        cp("act", B.ckvT[:], pb[:, 0:256].rearrange("p (k t) -> p k t", k=2), pk(b), ["ckvT"])
        if is_sample:
            return
        b = bank()
        pb = ps[:, b, :].bitcast(BF16)
        for h in range(NH):
            tr(pb[0:96, h * 128:(h + 1) * 128], B.qs[:, h, :], ident_b[:], ["qs", "ident_b"], pk(b))
        cp("act", B.qT[:, :, sub * 128:(sub + 1) * 128], pb[0:96, :].rearrange("p (h t) -> p h t", h=NH),
           pk(b), ["qT"])
        bkn, bvl = bank(), bank()
        for (bb, w) in ((bkn, w_uk_b), (bvl, w_uv_b)):
            for kt in range(2):
                mm(ps[:, bb, :], B.ckvT[:, kt, :], w[:, kt, :], kt == 0, kt == 1, ["ckvT", "w_uk_b", "w_uv_b"],
                   pk(bb))
        cp("dve", B.ks[:, :, 0:64], ps[:, bkn, :].rearrange("p (h d) -> p h d", h=NH), pk(bkn), ["ks"])
        cp("pool", B.ks[:, :, 64:96], B.krr[:].unsqueeze(1).to_broadcast([128, NH, ROPE]), ["krr"], ["ks"])
        ktile = tok0 // 128
        cp("act", B.Vaug[:, ktile, :, 0:64], ps[:, bvl, :].rearrange("p (h d) -> p h d", h=NH), pk(bvl),
           ["Vaug"])
        b = bank()
        pb = ps[:, b, :].bitcast(BF16)
        for h in range(NH):
            tr(pb[0:96, h * 128:(h + 1) * 128], B.ks[:, h, :], ident_b[:], ["ks", "ident_b"], pk(b))
        cp("dve", B.kT[:, :, tok0:tok0 + 128], pb[0:96, :].rearrange("p (h t) -> p h t", h=NH), pk(b), ["kT"])
        cp("pool", B.v_bf[:], B.vg[:], ["vg"], ["v_bf"])
        b = bank()
        for h in range(NH):
            mm(ps[:, b, h * 64:(h + 1) * 64], wsT[:, h, :], B.v_bf[:, h * 64:(h + 1) * 64], True, True,
               ["wsT", "v_bf"], pk(b))
        tt("dve", B.gate_t[:].rearrange("p (h d) -> p h d", h=NH),
           ps[:, b, :].rearrange("p (h d) -> p h d", h=NH),
           bspT[:].unsqueeze(2).to_broadcast([128, NH, 64]), ALU.add, pk(b) + ["bspT"], ["gate_t"])
        tt("dve", B.gate_t[:], B.gate_t[:], B.u_sb[:], ALU.mult, ["gate_t", "u_sb"], ["gate_t"])
        act(B.junk[:, 0:D_A], B.gate_t[:], AF.Square, ["gate_t"], ["junk", "sm4"], accum=B.sm[:, 4:5])
        rms_scale(B.sm[:, 4:5], B.sm[:, 4:5], D_A, ["sm4"], ["sm4"])
        ts("dve", B.mixbf[:, sub, 0:D_A], B.gate_t[:], B.sm[:, 4:5], None, ALU.mult, None, ["gate_t", "sm4"],
           ["mixbf%d" % sub])

    def attention(blk):
        nkt = NSUB * blk + NSUB
        for h in range(NH):
            bo = 6 + (h % 2)
            for kt in range(nkt):
                i = kt - NSUB * blk
                c0 = 128 * i if i > 0 else 0
                n = TB - c0
                bs = bank()
                mm(ps[:, bs, 0:n], B.kT[:, h, kt * 128:(kt + 1) * 128], B.qT[:, h, c0:TB], True, True,
                   ["kT", "qT"], pk(bs))
                pi = rr["pt"] % 3
                rr["pt"] += 1
                pkey = "pt%d" % pi
                pt_ = B.ptile[pi]
                act(pt_[:, 0:n], ps[:, bs, 0:n], AF.Exp, pk(bs), [pkey], scale=ATTN_SCALE)
                if i >= 0:
                    tt("pool", pt_[:, 0:128], pt_[:, 0:128], maskT[:], ALU.mult, [pkey, "maskT"], [pkey])
                for j in range(max(i, 0), NSUB):
                    cj = (j * 128) - c0
                    mm(ps[:, bo, j * 65:(j + 1) * 65], pt_[:, cj:cj + 128], B.Vaug[:, kt, h, :],
                       kt == 0 and j == 0, kt == NSUB * blk + j, [pkey, "Vaug"], pk(bo), skip=True)
            po = ps[:, bo, 0:NSUB * 65].rearrange("p (j d) -> p j d", j=NSUB)
            cp("act", B.o_sb[:], po, pk(bo), ["o_sb"])
            recip(B.rden[:], B.o_sb[:, :, 64], ["o_sb"], ["rden"])
            tt("dve", B.attn_o[:, :, h * 64:(h + 1) * 64], B.o_sb[:, :, 0:64],
               B.rden[:].unsqueeze(2).to_broadcast([128, NSUB, 64]), ALU.mult, ["o_sb", "rden"], ["attn_o"])

    def post(sub, is_sample=False, mix_src=None):
        xk = None
        if not is_sample:
            act(B.junk[:, 0:D_A], B.attn_o[:, sub, :], AF.Square, ["attn_o"], ["junk", "sm5"], accum=B.sm[:, 5:6])
            rms_scale(B.sm[:, 5:6], B.sm[:, 5:6], D_A, ["sm5"], ["sm5"])
            ts("dve", B.mixbf[:, sub, D_A:D], B.attn_o[:, sub, :], B.sm[:, 5:6], None, ALU.mult, None,
               ["attn_o", "sm5"], ["mixbf%d" % sub])
        b = bank()
        pb = ps[:, b, :].bitcast(BF16)
        for kt in range(KD):
            tr(pb[:, kt * 128:(kt + 1) * 128], B.mixbf[:, sub, kt * 128:(kt + 1) * 128], ident_b[:],
               ["mixbf%d" % sub, "ident_b"], pk(b))
        cp("act", B.mT[:], pb.rearrange("p (k t) -> p k t", k=KD), pk(b), ["mT"])
        b = bank(2)
        for half in range(2):
            for kt in range(KD):
                mm(ps[:, b + half, :], B.mT[:, kt, :], w_out_b[:, kt, half * 512:(half + 1) * 512], kt == 0,
                   kt == KD - 1, ["mT", "w_out_b"], pk(b + half))
        return b

    def post2(sub, b, tcol):
        for half in range(2):
            tt("dve", B.x2[:, sub, half * 512:(half + 1) * 512], ps[:, b + half, :],
               B.x2[:, sub, half * 512:(half + 1) * 512], ALU.add, pk(b + half) + ["x2_%d" % sub],
               ["x2_%d" % sub])
        act(B.junk[:], B.x2[:, sub, :], AF.Square, ["x2_%d" % sub], ["junk", "sm6"], accum=B.sm[:, 6:7])
        rms_scale(B.sm[:, 6:7], B.sm[:, 6:7], D, ["sm6"], ["sm6"])
        ts("pool", B.mixbf[:, sub, :], B.x2[:, sub, :], B.sm[:, 6:7], None, ALU.mult, None, ["x2_%d" % sub, "sm6"],
           ["mixbf%d" % sub])
        bb = bank()
        pb = ps[:, bb, :].bitcast(BF16)
        for kt in range(KD):
            tr(pb[:, kt * 128:(kt + 1) * 128], B.mixbf[:, sub, kt * 128:(kt + 1) * 128], ident_b[:],
               ["mixbf%d" % sub, "ident_b"], pk(bb))
        cp("act", B.h2T[:, :, 2 + tcol:2 + tcol + 128], pb.rearrange("p (k t) -> p k t", k=KD), pk(bb), ["h2T"])

    def ffn_up_sample():
        ntok = 128
        stv = st.rearrange("b j (gv f) -> b j gv f", gv=2)
        ocv = o_sconv[:, 1, :].rearrange("b (gv f) -> b gv f", gv=2)
        for ft in range(NFT):
            wi = rr["wu"] % 2
            rr["wu"] += 1
            wk = "wu%d" % wi
            dma(B.wu[wi][:].rearrange("p g k f -> p (g k f)"), wup_s[ft], ["wup_s"], [wk], wk)
            dma(B.stf[:], stv[:, :, :, ft * 128:(ft + 1) * 128], [], ["qf"], "qf")
            for gv, dst, dk in ((0, B.upg, "upg"), (1, B.upv, "upv")):
                b = bank()
                for kt in range(KD):
                    mm(ps[:, b, 0:ntok], B.wu[wi][:, gv, kt, :], B.h2T[:, kt, 2:2 + ntok], kt == 0, kt == KD - 1,
                       [wk, "h2T"], pk(b))
                cp("act", dst[:, 0:ntok], ps[:, b, 0:ntok], pk(b), [dk])
            bU = bank()
            for gv, src, dk, dst, ck in ((0, B.upg, "upg", B.cg, "cg"), (1, B.upv, "upv", B.cv, "cv")):
                f = gv * NFT + ft
                bT = bank()
                for j in range(2):
                    tr(ps[:, bT, j * 128:(j + 1) * 128], B.stf[:, j, gv, :], ident_f[:], ["qf", "ident_f"], pk(bT))
                act(dst[:, 0:ntok], src[:, 0:ntok], AF.Identity, [dk, "wc", "bc"], [ck], scale=wc[:, 2, f:f + 1],
                    bias=bc[:, f:f + 1])
                stt(dst[:, 0:ntok], ps[:, bT, 128:256], wc[:, 1, f:f + 1], dst[:, 0:ntok], ALU.mult, ALU.add,
                    pk(bT) + ["wc", ck], [ck])
                stt(dst[:, 0:ntok], ps[:, bT, 0:128], wc[:, 0, f:f + 1], dst[:, 0:ntok], ALU.mult, ALU.add,
                    pk(bT) + ["wc", ck], [ck])
                tr(ps[:, bU, gv * 128:(gv + 1) * 128], src[:, 0:ntok], ident_f[:], [dk, "ident_f"], pk(bU))
            cp("dve", B.upn[:], ps[:, bU, 0:256].rearrange("p (g f) -> p g f", g=2), pk(bU), ["gate_t"])
            dma(ocv[:, :, ft * 128:(ft + 1) * 128], B.upn[:], ["gate_t"], [], "gate_t", is_out=True)
            act(B.cg[:, 0:ntok], B.cg[:, 0:ntok], AF.Silu, ["cg"], ["cg"])
            tt("dve", B.actT[:, ft, 0:ntok], B.cg[:, 0:ntok], B.cv[:, 0:ntok], ALU.mult, ["cg", "cv"], ["actT"])

    def ffn_up(ntok, first_in_seq, st_fn=None, save_last=None):
        if first_in_seq:
            memset("pool", B.h2T[:, :, 0:2], 0.0, ["h2T"])
        N = ntok + 2
        for ft in range(NFT):
            wi = rr["wu"] % 2
            rr["wu"] += 1
            wk = "wu%d" % wi
            dma(B.wu[wi][:].rearrange("p g k f -> p (g k f)"), wup_s[ft], ["wup_s"], [wk], wk)
            for gv, dst, dk in ((0, B.upg, "upg"), (1, B.upv, "upv")):
                b = bank()
                for kt in range(KD):
                    mm(ps[:, b, 0:N], B.wu[wi][:, gv, kt, :], B.h2T[:, kt, 0:N], kt == 0, kt == KD - 1,
                       [wk, "h2T"], pk(b))
                cp("act", dst[:, 0:N], ps[:, b, 0:N], pk(b), [dk])
            for gv, src, dk, dst, ck in ((0, B.upg, "upg", B.cg, "cg"), (1, B.upv, "upv", B.cv, "cv")):
                f = gv * NFT + ft
                if st_fn is not None:
                    st_fn(ft, gv, src)
                act(dst[:, 0:ntok], src[:, 2:N], AF.Identity, [dk, "wc", "bc"], [ck], scale=wc[:, 2, f:f + 1],
                    bias=bc[:, f:f + 1])
                stt(dst[:, 0:ntok], src[:, 1:N - 1], wc[:, 1, f:f + 1], dst[:, 0:ntok], ALU.mult, ALU.add,
                    [dk, "wc", ck], [ck])
                stt(dst[:, 0:ntok], src[:, 0:N - 2], wc[:, 0, f:f + 1], dst[:, 0:ntok], ALU.mult, ALU.add,
                    [dk, "wc", ck], [ck])
                if save_last is not None:
                    cp("pool", B.halo[:, f, :], src[:, N - 2:N], [dk], ["halo"])
            act(B.cg[:, 0:ntok], B.cg[:, 0:ntok], AF.Silu, ["cg"], ["cg"])
            tt("dve", B.actT[:, ft, 0:ntok], B.cg[:, 0:ntok], B.cv[:, 0:ntok], ALU.mult, ["cg", "cv"], ["actT"])
        cp("pool", B.h2T[:, :, 0:2], B.h2T[:, :, ntok:ntok + 2], ["h2T"], ["h2T"])

    def ffn_down(nsub, out_fn):
        bs = [bank(2) for _ in range(nsub)]
        for g in range(NFT // 2):
            wi = rr["wd"] % 2
            rr["wd"] += 1
            wk = "wd%d" % wi
            dma(B.wd[wi][:].rearrange("p t n -> p (t n)"), wdn_s[g], ["wdn_s"], [wk], wk)
            for t2 in range(2):
                kt = 2 * g + t2
                for s_ in range(nsub):
                    for half in range(2):
                        mm(ps[:, bs[s_] + half, :], B.actT[:, kt, s_ * 128:(s_ + 1) * 128],
                           B.wd[wi][:, t2, half * 512:(half + 1) * 512], kt == 0, kt == NFT - 1,
                           [wk, "actT"], pk(bs[s_] + half))
        for s_ in range(nsub):
            out_fn(s_, bs[s_])

    def final_out(sub, b, dst_ap):
        yk = "x2_%d" % sub
        y = B.x2[:, sub, :]
        for half in range(2):
            tt("dve", y[:, half * 512:(half + 1) * 512], ps[:, b + half, :],
               y[:, half * 512:(half + 1) * 512], ALU.add, pk(b + half) + [yk], [yk])
        act(B.junk[:], y, AF.Square, [yk], ["junk", "sm7"], accum=B.sm[:, 7:8])
        rms_scale(B.sm[:, 7:8], B.sm[:, 7:8], D, ["sm7"], ["sm7"])
        stt(y, y, B.sm[:, 7:8], Gfin[:], ALU.mult, ALU.mult, [yk, "sm7", "Gfin"], [yk])
        dma(dst_ap, y, [yk], [], yk, is_out=True)


    if cfg.do_sample:
        P.barrier()
        arena["off"] = 0
        NPJ = NPAGES
        Qall = sb("Qall", [128, NH, KVR + ROPE], BF16)
        E_all = sb("E_all", [128, NGRP, 128], BF16); E_f = sb("E_f", [128, NGRP, 128])
        ET_all = sb("ET_all", [128, NGRP, 128], BF16)
        off_p = arena["off"]
        alloc_front()
        qnT = sb("qnT", [64, NH, 128], BF16)
        big = sb("big", [128, NH, KVR + ROPE])
        snew = sb("snew", [128, NH])
        pt_i = sb("pt_i", [128, NPJ], I32); pt_f = sb("pt_f", [128, NPJ])
        LT = sb("LT", [128, 128]); hi128 = sb("hi128", [128, 128]); lo_t = sb("lo_t", [128, 128])
        Gge = sb("Gge", [128, 128, NGRP]); Glt = sb("Glt", [128, 128, NGRP])
        G_bf = sb("G_bf", [128, 128, NGRP], BF16)
        RT = sb("RT", [128, 128, 128], BF16)

        front(xs, c_coss, c_sins, True, lat_out=o_slat, kr_out=o_skr, v_out=o_sv)
        g3 = B.gate_t[:].rearrange("p (h d) -> p h d", h=NH)
        tt("dve", g3, B.vg[:].rearrange("p (h d) -> p h d", h=NH), W0[:].unsqueeze(2).to_broadcast([128, NH, 64]),
           ALU.mult, ["vg", "W0"], ["gate_t"])
        tt("dve", g3, g3, B0[:].unsqueeze(2).to_broadcast([128, NH, 64]), ALU.add, ["gate_t", "B0"], ["gate_t"])
        tt("dve", B.gate_t[:], B.gate_t[:], B.u_sb[:], ALU.mult, ["gate_t", "u_sb"], ["gate_t"])
        act(B.junk[:, 0:D_A], B.gate_t[:], AF.Square, ["gate_t"], ["junk", "sm4"], accum=B.sm[:, 4:5])
        rms_scale(B.sm[:, 4:5], B.sm[:, 4:5], D_A, ["sm4"], ["sm4"])
        ts("dve", s_mix[:], B.gate_t[:], B.sm[:, 4:5], None, ALU.mult, None, ["gate_t", "sm4"], ["s_mix"])
        cp("pool", cnew_f[:, 0:KVR], B.ckv[:], ["ckv"], ["cnew_f"])
        cp("pool", cnew_f[:, KVR:KVR + ROPE], B.krr[:], ["krr"], ["cnew_f"])
        b = bank()
        pb = ps[:, b, :].bitcast(BF16)
        for h in range(NH):
            tr(pb[0:64, h * 128:(h + 1) * 128], B.qs[:, h, 0:64], ident_b[:], ["qs", "ident_b"], pk(b))
        cp("act", qnT[:], pb[0:64, :].rearrange("p (h t) -> p h t", h=NH), pk(b), ["qnT"])
        for h2 in range(NH // 2):
            bb = bank()
            for hh in range(2):
                h = 2 * h2 + hh
                mm(ps[:, bb, hh * KVR:(hh + 1) * KVR], qnT[:, h, :], w_ukT[:, h, :], True, True,
                   ["qnT", "w_ukT"], pk(bb))
            cp("act" if h2 % 2 else "dve", Qall[:, 2 * h2:2 * h2 + 2, 0:KVR],
               ps[:, bb, :].rearrange("p (h r) -> p h r", h=2), pk(bb), ["Qall"])
        cp("pool", Qall[:, :, KVR:KVR + ROPE], B.qs[:, :, 64:96], ["qs"], ["Qall"])
        tt("dve", big[:], Qall[:], cnew_f[:].unsqueeze(1).to_broadcast([128, NH, KVR + ROPE]), ALU.mult,
           ["Qall", "cnew_f"], ["big"])
        P.add("dve", lambda e: e.tensor_reduce(snew[:], big[:], AX.X, ALU.add), ["big"], ["snew"])
        act(e_new[:], snew[:], AF.Exp, ["snew"], ["e_new"], scale=ATTN_SCALE)

        try:
            if "E" not in cfg.stages:
                raise StopIteration
            dma(pt_i[:], ptab, [], ["pt_i"], "pt_i")
            cp("dve", pt_f[:], pt_i[:], ["pt_i"], ["pt_f"])
            ts("dve", pt_f[:], pt_f[:], pgbase[:, 0:1], None, ALU.subtract, None, ["pt_f", "pgbase"], ["pt_f"])
            b = bank()
            if cfg.estop < 1:
                raise StopIteration
            tr(ps[0:NPJ, b, 0:128], pt_f[:], ident_f[:], ["pt_f", "ident_f"], pk(b))
            cp("dve", LT[0:NPJ, :], ps[0:NPJ, b, 0:128], pk(b), ["LT"])
            LTb = LT[0:NPJ, :].unsqueeze(2).to_broadcast([NPJ, 128, NGRP])
            glob = glo[0:NPJ, :].unsqueeze(1).to_broadcast([NPJ, 128, NGRP])
            if cfg.estop < 2:
                raise StopIteration
            tt("dve", Gge[0:NPJ], LTb, glob, ALU.is_ge, ["LT", "glo"], ["Gge"])
            tt("dve", Glt[0:NPJ], LTb, glob, ALU.subtract, ["LT", "glo"], ["Glt"])
            ts("dve", Glt[0:NPJ], Glt[0:NPJ], 128.0, None, ALU.is_lt, None, ["Glt"], ["Glt"])
            tt("dve", Gge[0:NPJ], Gge[0:NPJ], Glt[0:NPJ], ALU.mult, ["Gge", "Glt"], ["Gge"])
            cp("dve", G_bf[0:NPJ], Gge[0:NPJ], ["Gge"], ["G_bf"])
            if cfg.estop < 3:
                raise StopIteration
            tt("dve", Glt[0:NPJ], Gge[0:NPJ], glob, ALU.mult, ["Gge", "glo"], ["Glt"])
            P.add("dve", lambda e: e.tensor_reduce(hi128[0:NPJ, :], Glt[0:NPJ], AX.X, ALU.add), ["Glt"], ["hi128"])
            tt("dve", lo_t[0:NPJ, :], LT[0:NPJ, :], hi128[0:NPJ, :], ALU.subtract, ["LT", "hi128"], ["lo_t"])
            if cfg.estop < 4:
                raise StopIteration
            tt("dve", RT[0:NPJ], lo_t[0:NPJ, :].unsqueeze(2).to_broadcast([NPJ, 128, 128]),
               iota_r[0:NPJ, :].unsqueeze(1).to_broadcast([NPJ, 128, 128]), ALU.is_equal, ["lo_t", "iota_r"], ["RT"])
            if cfg.estop < 5:
                raise StopIteration
            CB = 512 // NGRP
            for b0 in range(0, 128, CB):
                nb = min(CB, 128 - b0)
                bb = bank()
                for bi in range(nb):
                    mm(ps[:, bb, bi * NGRP:(bi + 1) * NGRP], RT[0:NPJ, b0 + bi, :], G_bf[0:NPJ, b0 + bi, :], True, True,
                       ["RT", "G_bf"], pk(bb))
                src = ps[:, bb, 0:nb * NGRP].rearrange("p (b g) -> p g b", g=NGRP)
                cp("act", E_all[:, :, b0:b0 + nb], src, pk(bb), ["E_all"])
                cp("dve", E_f[:, :, b0:b0 + nb], src, pk(bb), ["E_f"])
            if cfg.estop < 6:
                raise StopIteration
            for g0 in range(0, NGRP, 8):
                ng = min(8, NGRP - g0)
                bb = bank()
                pb = ps[:, bb, :].bitcast(BF16)
                for gi in range(ng):
                    tr(pb[:, gi * 128:(gi + 1) * 128], E_all[:, g0 + gi, :], ident_b[:], ["E_all", "ident_b"], pk(bb))
                cp("act", ET_all[:, g0:g0 + ng, :], pb[:, 0:ng * 128].rearrange("p (g t) -> p g t", g=ng), pk(bb),
                   ["ET_all"])
        except StopIteration:
            pass
        memset("pool", acc_sb[:], 0.0, ["acc_sb"])

        P.barrier()
        arena["off"] = off_p
        QselT = sb("QselT", [128, 3, 128, NH], BF16)
        Xf = [sb("Xf%d" % i, [128, 4, KVR + ROPE]) for i in range(3)]
        Xb = [sb("Xb%d" % i, [128, 4, KVR + ROPE], BF16) for i in range(8)]
        XT = [sb("XT%d" % i, [128, 3, 4, 128], BF16) for i in range(2)]
        et_all = [sb("et%d" % i, [128, 128, NH], BF16) for i in range(2)]
        OT_sb = sb("OT_sb", [128, 2, NH, 128])
        Opg = sb("Opg", [128, NH, 257])
        xrr = dict(xf=0, xb=0, xt=0, ev=0)
        psn[0] = 4
        psctr[0] = 0
        for g in range(NGRP if "loop" in cfg.stages else 0):
            for k, (c0, rows) in enumerate(((0, 128), (128, 128), (256, ROPE))):
                bq = bank(2)
                for h in range(NH):
                    mm(ps[0:rows, bq + h // 4, (h % 4) * 128:(h % 4 + 1) * 128], Qall[:, h, c0:c0 + rows],
                       ET_all[:, g, :], True, True, ["Qall", "ET_all"], pk(bq + h // 4))
                for hb in range(2):
                    cp("act" if hb else "dve", QselT[0:rows, k, :, hb * 4:(hb + 1) * 4],
                       ps[0:rows, bq + hb, :].rearrange("p (h q) -> p q h", h=4), pk(bq + hb), ["QselT"])
            et = et_all[g % 2]
            ek = "et%d" % (g % 2)
            for sbi in range(8):
                bS = 4 + sbi % 2
                slots = []
                for q4 in range(4):
                    pg0 = g * 128 + sbi * 16 + q4 * 4
                    xi = xrr["xf"] % 3; xrr["xf"] += 1
                    xk = "Xf%d" % xi
                    dma(Xf[xi][:, :, 0:KVR], lat[pg0:pg0 + 4].rearrange("n t r -> t n r"), [], [xk], xk)
                    dma(Xf[xi][:, :, KVR:KVR + ROPE], krc[pg0:pg0 + 4].rearrange("n t r -> t n r"), [], [xk], xk)
                    si = xrr["xb"] % 8; xrr["xb"] += 1
                    bk_ = "Xb%d" % si
                    slots.append(si)
                    cp("pool", Xb[si][:], Xf[xi][:], [xk], [bk_])
                    bA, bB = bank(), bank()
                    pbA = ps[:, bA, :].bitcast(BF16)
                    pbB = ps[:, bB, :].bitcast(BF16)
                    for pg in range(4):
                        for c in range(2):
                            tr(pbA[:, (c * 4 + pg) * 128:(c * 4 + pg + 1) * 128], Xb[si][:, pg, c * 128:(c + 1) * 128],
                               ident_b[:], [bk_, "ident_b"], pk(bA))
                        tr(pbB[0:ROPE, pg * 128:(pg + 1) * 128], Xb[si][:, pg, KVR:KVR + ROPE], ident_b[:],
                           [bk_, "ident_b"], pk(bB))
                    ti = xrr["xt"] % 2; xrr["xt"] += 1
                    tk_ = "XT%d" % ti
                    ev = xrr["ev"] % 2; xrr["ev"] += 1
                    cp("act" if ev else "dve", XT[ti][:, 0:2, :, :],
                       pbA.rearrange("p (c n t) -> p c n t", c=2, n=4), pk(bA), [tk_ + "a"])
                    cp("dve" if ev else "act", XT[ti][0:ROPE, 2, :, :],
                       pbB[0:ROPE, 0:512].rearrange("p (n t) -> p n t", n=4), pk(bB), [tk_ + "b"])
                    for pg in range(4):
                        p = sbi * 16 + q4 * 4 + pg
                        col = (q4 * 4 + pg) * NH
                        for k, rows in enumerate((128, 128, ROPE)):
                            mm(ps[:, bS, col:col + NH], XT[ti][0:rows, k, pg, :], QselT[0:rows, k, p, :], k == 0,
                               k == 2, [tk_ + "a", tk_ + "b", "QselT"], pk(bS))
                act(et[:, sbi * 16:(sbi + 1) * 16, :].rearrange("p n h -> p (n h)"), ps[:, bS, 0:16 * NH], AF.Exp,
                    pk(bS), [ek], scale=ATTN_SCALE)
                hp = sbi % 4
                for q4 in range(4):
                    si = slots[q4]
                    for pg in range(4):
                        p = sbi * 16 + q4 * 4 + pg
                        pc = (p % 64) * NH
                        for c in range(2):
                            mm(ps[:, 6 + c, pc:pc + NH], Xb[si][:, pg, c * 128:(c + 1) * 128], et[:, p, :], True, True,
                               ["Xb%d" % si, ek], pk(6 + c))
                if sbi % 4 == 3:
                    half = sbi // 4
                    for c in range(2):
                        cp("act" if c else "dve", OT_sb[:, c, :, half * 64:(half + 1) * 64],
                           ps[:, 6 + c, :].rearrange("p (q h) -> p h q", h=NH), pk(6 + c), ["OT_sb"])
            bL = bank()
            for h in range(NH):
                mm(ps[:, bL, h:h + 1], et[:, :, h], ones_b[:], True, True, [ek, "ones_b"], pk(bL))
            cp("dve", Opg[:, :, 256], ps[:, bL, 0:NH], pk(bL), ["Opg"])
            for hq in range(2):
                bt = bank(2)
                for hh in range(4):
                    h = hq * 4 + hh
                    for c in range(2):
                        col = ((hh % 2) * 2 + c) * 128
                        tr(ps[:, bt + hh // 2, col:col + 128], OT_sb[:, c, h, :], ident_f[:], ["OT_sb", "ident_f"],
                           pk(bt + hh // 2))
                for i2 in range(2):
                    cp("act" if i2 else "dve", Opg[:, hq * 4 + 2 * i2:hq * 4 + 2 * i2 + 2, 0:256],
                       ps[:, bt + i2, :].rearrange("p (h r) -> p h r", h=2), pk(bt + i2), ["Opg"])
            opf = Opg[:].rearrange("p h r -> p (h r)")
            acf = acc_sb[:].rearrange("p h r -> p (h r)")
            for c0 in range(0, NH * 257, 512):
                n = min(512, NH * 257 - c0)
                bc_ = bank()
                mm(ps[:, bc_, 0:n], E_f[:, g, :], opf[:, c0:c0 + n], True, True, ["E_f", "Opg"], pk(bc_))
                tt("dve", acf[:, c0:c0 + n], acf[:, c0:c0 + n], ps[:, bc_, 0:n], ALU.add, pk(bc_) + ["acc_sb"],
                   ["acc_sb"])
        if cfg.debug:
            dma(dbg_acc, acc_sb[:].rearrange("p h r -> p (h r)"), ["acc_sb"], [], "dbg3", is_out=True)
            dma(dbg_E, E_f[:].rearrange("p g b -> p (g b)"), ["E_f"], [], "dbg4", is_out=True)
            dma(dbg_en, e_new[:], ["e_new"], [], "dbg6", is_out=True)
        psn[0] = 6
        psctr[0] = 0
        dma(accd_in, acc_sb[:].rearrange("p h r -> p (h r)"), ["acc_sb"], ["accd_in"], "accd")
        if cfg.ncores > 1 and "cc" in cfg.stages:
            P.add("pool", lambda e: e.collective_compute("AllReduce", ALU.add, replica_groups=[list(range(cfg.ncores))],
                                                         ins=[accd_in], outs=[accd_out]),
                  ["accd_in"], ["accd_out"], dma=True, semkey="cc", inc=1)
        else:
            dma(accd_out, accd_in, ["accd_in"], ["accd_out"], "cc")

    P.barrier()
    arena["off"] = 0
    alloc_main()
    memset("pool", B.Vaug[:], 1.0, ["Vaug"])
    for sq in range(NSEQ if "prompt" in cfg.stages else 0):
        for blk in range(NBLK):
            t0 = blk * TB
            for sub in range(NSUB):
                tk = t0 + sub * 128
                front(xp[sq, tk:tk + 128, :], c_cosp[tk:tk + 128, :], c_sinp[tk:tk + 128, :], False, sub, tk,
                      lat_out=o_plat[sq, tk:tk + 128, :], kr_out=o_pkr[sq, tk:tk + 128, :])
            attention(blk)
            for sub in range(NSUB):
                b = post(sub)
                post2(sub, b, sub * 128)
            last = blk == NBLK - 1
            ffn_up(TB, blk == 0, save_last=True if last else None)
            if last:
                for j in range(2):
                    for gv in range(2):
                        dma(o_pconv[sq, j, gv * D_FF:(gv + 1) * D_FF].rearrange("(f p) -> p f", p=128),
                            B.halo[:, gv * NFT:(gv + 1) * NFT, j], ["halo"], [], "halo", is_out=True, slow=True)

            def outf(s_, b, sq=sq, t0=t0):
                final_out(s_, b, yp[sq, t0 + s_ * 128:t0 + (s_ + 1) * 128, :])
            ffn_down(NSUB, outf)


    if cfg.do_sample and "s3" in cfg.stages:
        acf = acc_sb[:].rearrange("p h r -> p (h r)")
        dma(acf, accd_out, ["accd_out"], ["acc_sb"], "accd")
        tmp3 = B.x2[:, 0:2, :].rearrange("p a (h r) -> p (a h) r", h=4)
        cb3 = cnew_f[:, 0:KVR].unsqueeze(1).to_broadcast([128, NH, KVR])
        eb3 = e_new[:].unsqueeze(2).to_broadcast([128, NH, KVR])
        tt("dve", tmp3, cb3, eb3, ALU.mult, ["cnew_f", "e_new"], ["x2_0", "x2_1"])
        tt("dve", acc_sb[:, :, 0:KVR], acc_sb[:, :, 0:KVR], tmp3, ALU.add, ["acc_sb", "x2_0", "x2_1"], ["acc_sb"])
        tt("dve", acc_sb[:, :, 256], acc_sb[:, :, 256], e_new[:], ALU.add, ["acc_sb", "e_new"], ["acc_sb"])
        rl = B.sm[:, 8:16]
        recip(rl, acc_sb[:, :, 256], ["acc_sb"], ["sm8"])
        olat_bf = B.mixbf[:, 0:2, :].rearrange("p a (h r) -> p (a h) r", h=4)
        tt("dve", olat_bf, acc_sb[:, :, 0:KVR], rl.unsqueeze(2).to_broadcast([128, NH, KVR]), ALU.mult,
           ["acc_sb", "sm8"], ["mixbf0", "mixbf1"])
        olT = B.actT[:, 0:NH, :]
        for hq in range(2):
            bb = bank()
            pb = ps[:, bb, :].bitcast(BF16)
            for hh in range(4):
                for c in range(2):
                    col = (hh * 2 + c) * 128
                    tr(pb[:, col:col + 128], olat_bf[:, hq * 4 + hh, c * 128:(c + 1) * 128], ident_b[:],
                       ["mixbf0", "mixbf1", "ident_b"], pk(bb))
            cp("act", olT[:, hq * 4:hq * 4 + 4, :], pb.rearrange("p (h x) -> p h x", h=4), pk(bb), ["actT"])
        bo_ = bank()
        for h in range(NH):
            for c in range(2):
                mm(ps[:, bo_, h * 64:(h + 1) * 64], olT[:, h, c * 128:(c + 1) * 128],
                   w_uv_b[:, c, h * 64:(h + 1) * 64], c == 0, c == 1, ["actT", "w_uv_b"], pk(bo_), skip=True)
        cp("act", B.attn_o[:, 0, :], ps[:, bo_, :], pk(bo_), ["attn_o"])
        cp("pool", B.mixbf[:, 0, 0:D_A], s_mix[:], ["s_mix"], ["mixbf0"])
        dma(B.x2[:, 0, :], xs, [], ["x2_0"], "x2_0")
        bpo = post(0)
        post2(0, bpo, 0)
        dma(o_sconv[:, 0, :], st[:, 1, :], [], [], "sconv0", is_out=True)
        ffn_up_sample()

        def outs(s_, b):
            final_out(s_, b, ys)
        ffn_down(1, outs)

    P.emit(nc, es)
    es.close()
    return nc, P


def rope_tables(pos):
    inv = (np.float32(10000.0) ** (-np.arange(HALF, dtype=np.float32) / np.float32(HALF))).astype(np.float32)
    ang = pos.astype(np.float32)[:, None] * inv[None, :]
    return np.cos(ang).astype(np.float32), np.sin(ang).astype(np.float32)


def make_in_maps(cfg, inputs):
    nseq, seq = cfg.nseq, cfg.seq
    f = lambda a: np.ascontiguousarray(np.asarray(a, dtype=np.float32))
    cosp, sinp = rope_tables(np.arange(seq))
    kk = np.arange(128)
    common = dict(
        g_mix=f(inputs["g_mix"][0]), w_in=f(inputs["w_in"][0]), g_sgu=f(inputs["g_sgu"][0]),
        w_sp=f(inputs["w_spatial"][0]), b_sp=f(inputs["b_spatial"][0]), g_q=f(inputs["g_q"][0]),
        w_uq=f(inputs["w_uq"][0]).reshape(QR, NH * 96), g_kv=f(inputs["g_kv"][0]),
        w_uk=f(inputs["w_uk"][0]).reshape(KVR, NH * 64), w_uv=f(inputs["w_uv"][0]).reshape(KVR, NH * 64),
        g_oa=f(inputs["g_out_a"][0]), g_ob=f(inputs["g_out_b"][0]), w_out=f(inputs["w_out"][0]),
        g_ffn=f(inputs["g_ffn"][0]), w_up=f(inputs["w_up"][0]), w_conv=f(inputs["w_conv"][0]),
        b_conv=f(inputs["b_conv"][0]), w_down=f(inputs["w_down"][0]), g_fin=f(inputs["g_final"]),
        c_ident=np.eye(128, dtype=np.float32),
        c_mask=(kk[:, None] <= kk[None, :]).astype(np.float32),
        c_cosp=cosp, c_sinp=sinp,
    )
    maps = []
    xpr = f(inputs["x_prompt"])
    if cfg.do_sample:
        npl, ngrp = cfg.npl, cfg.npl // 128
        past_len = cfg.npages * 128
        coss, sins = rope_tables(np.full((NS,), past_len))
        common.update(
            xs=f(inputs["x_sample"][:, 0, :]), st=f(inputs["state_ffn_conv"][0]),
            ptab=np.ascontiguousarray(np.asarray(inputs["page_table"], dtype=np.int32)),
            c_coss=coss, c_sins=sins,
            c_iota=np.tile(np.arange(128, dtype=np.float32)[None, :], (128, 1)),
            c_glo=np.tile((128.0 * np.arange(ngrp, dtype=np.float32))[None, :], (128, 1)),
        )
        latc = inputs["cache_kv_latent"][0]
        krcc = inputs["cache_k_rope"][0]
    for c in range(cfg.ncores):
        m = dict(common)
        m["xp"] = xpr[c * nseq:(c + 1) * nseq]
        if cfg.do_sample:
            m["lat"] = f(latc[c * npl:(c + 1) * npl])
            m["krc"] = f(krcc[c * npl:(c + 1) * npl])
            m["c_pgbase"] = np.full((128, 1), float(c * npl), dtype=np.float32)
        maps.append(m)
    return maps


_CACHE = {}


def kernel(**inputs):
    cfg = Cfg()
    if "nc" not in _CACHE:
        _CACHE["nc"] = build(cfg)[0]
    nc = _CACHE["nc"]
    maps = make_in_maps(cfg, inputs)
    res = run_bass_kernel_spmd(nc, maps, core_ids=list(range(cfg.ncores)))
    r = res.results
    cat = lambda k: np.concatenate([r[c][k] for c in range(cfg.ncores)], axis=0)
    y_prompt = cat("yp")
    p_lat = cat("o_plat")[None]
    p_kr = cat("o_pkr")[None]
    p_conv = cat("o_pconv")[None]
    y_sample = r[0]["ys"][:, None, :]
    s_lat = r[0]["o_slat"][None, :, None, :]
    s_kr = r[0]["o_skr"][None, :, None, :]
    s_v = r[0]["o_sv"][None, :, None, :]
    s_conv = r[0]["o_sconv"][None]
    return (y_prompt, y_sample, p_lat, p_kr, p_conv, s_lat, s_kr, s_v, s_conv)
```

```python
import numpy as np
from contextlib import ExitStack
import concourse.bass as bass
import concourse.mybir as mybir
from concourse.bass_utils import run_bass_kernel_spmd

F32 = mybir.dt.float32
BF16 = mybir.dt.bfloat16
I32 = mybir.dt.int32
AF = mybir.ActivationFunctionType
ALU = mybir.AluOpType
AX = mybir.AxisListType

D = 1024
KD = 8
D_A = 512
QR = 384
KVR = 256
ROPE = 32
HALF = 16
D_IN = 1696
D_FF = 2816
NFT = 22
NH = 8
EPS = 1e-6
ATTN_SCALE = 96.0 ** -0.5
NCORES = 8
NS = 128
TB = 256
NSUB = TB // 128


class Cfg:
    def __init__(self, nseq=2, seq=2048, npl=2560, npages=128, ncores=8, do_sample=True, debug=False):
        self.debug = debug
        self.reserve = 512
        self.stages = ("s1", "E", "loop", "cc", "prompt", "s3")
        self.estop = 99
        self.swdge_cast = False
        self.pool_cast = False
        self.nseq = nseq
        self.seq = seq
        self.npl = npl
        self.npages = npages
        self.ncores = ncores
        self.do_sample = do_sample


class Prog:
    ENGS = ("pe", "act", "dve", "pool", "sp")

    def __init__(self):
        self.ops = []
        self.lastw = {}
        self.readers = {}
        self.lastdma = {}
        self.out_keys = set()
        self.bar = None
        self.bar_done = set()

    def barrier(self):
        last = {}
        for i, o in enumerate(self.ops):
            if not o["dma"]:
                last[o["eng"]] = i
        self.bar = set(last.values()) | set(self.lastdma.values())
        self.bar_done = set()

    def add(self, eng, fn, reads=(), writes=(), dma=False, semkey=None, is_out=False, inc=16):
        i = len(self.ops)
        psr = [k for k in reads if k.startswith("ps") and k not in writes]
        if psr:
            writes = list(writes) + psr
        deps = {}
        for k in reads:
            if k in self.lastw:
                deps[self.lastw[k]] = True
        for k in writes:
            if k in self.lastw:
                deps.setdefault(self.lastw[k], False)
            for r in self.readers.get(k, ()):
                deps.setdefault(r, False)
        if dma:
            assert semkey is not None
            if semkey in self.lastdma:
                deps.setdefault(self.lastdma[semkey], False)
            self.lastdma[semkey] = i
            if is_out:
                self.out_keys.add(semkey)
        for k in writes:
            self.lastw[k] = i
            self.readers[k] = []
        for k in reads:
            if k not in writes:
                self.readers.setdefault(k, []).append(i)
        if self.bar is not None and eng not in self.bar_done:
            self.bar_done.add(eng)
            for d in self.bar:
                deps[d] = True
        deps.pop(i, None)
        self.ops.append(dict(eng=eng, fn=fn, deps=deps, dma=dma, semkey=semkey, inc=inc))
        return i

    @staticmethod
    def _edge(p, o, raw):
        if p["dma"] or o["dma"] or p["eng"] != o["eng"]:
            return True
        return raw and p["eng"] != "pe"

    def emit(self, nc, es):
        ops = self.ops
        n = len(ops)
        need = [False] * n
        for i, o in enumerate(ops):
            for d, raw in o["deps"].items():
                if self._edge(ops[d], o, raw):
                    need[d] = True
        cnt = {e: 0 for e in self.ENGS}
        val = [0] * n
        dcnt = {}
        for i, o in enumerate(ops):
            if o["dma"]:
                k = o["semkey"]
                dcnt[k] = dcnt.get(k, 0) + o["inc"]
                val[i] = dcnt[k]
            elif need[i]:
                cnt[o["eng"]] += 1
                val[i] = cnt[o["eng"]]
        esem = {e: es.enter_context(nc.semaphore("s_" + e)) for e in self.ENGS}
        dsem = {}
        for k in dcnt:
            dsem[k] = es.enter_context(nc.semaphore("d%d" % len(dsem)))
        self.n_sems = len(esem) + len(dsem)
        by_eng = {e: [i for i, o in enumerate(ops) if o["eng"] == e] for e in self.ENGS}
        out_final = {k: dcnt[k] for k in self.out_keys}

        def run(ename, eng):
            waited = {}
            for i in by_eng[ename]:
                o = ops[i]
                for d in sorted(o["deps"]):
                    p = ops[d]
                    if not self._edge(p, o, o["deps"][d]):
                        continue
                    if p["dma"]:
                        sem, v, sk = dsem[p["semkey"]], val[d], ("d", p["semkey"])
                    else:
                        sem, v, sk = esem[p["eng"]], val[d], ("e", p["eng"])
                    if waited.get(sk, 0) >= v:
                        continue
                    waited[sk] = v
                    eng.wait_ge(sem, v)
                inst = o["fn"](eng)
                if o["dma"]:
                    inst.then_inc(dsem[o["semkey"]], o["inc"])
                elif need[i]:
                    inst.then_inc(esem[ename], 1)
            if ename == "sp":
                for k, v in out_final.items():
                    eng.wait_ge(dsem[k], v)

        block = es.enter_context(nc.Block())

        @block.sync
        def _(e):
            run("sp", e)

        @block.tensor
        def _(e):
            run("pe", e)

        @block.scalar
        def _(e):
            run("act", e)

        @block.vector
        def _(e):
            run("dve", e)

        @block.gpsimd
        def _(e):
            run("pool", e)


def build(cfg):
    nc = bass.Bass("TRN2", target_bir_lowering=False)
    P = Prog()
    es = ExitStack()
    NSEQ, SEQ, NPL, NPAGES = cfg.nseq, cfg.seq, cfg.npl, cfg.npages
    NBLK = SEQ // TB
    NKT = SEQ // 128
    NGRP = NPL // 128

    def din(name, shape, dt=F32):
        return nc.dram_tensor(name, list(shape), dt, kind="ExternalInput").ap()

    def dout(name, shape, dt=F32):
        return nc.dram_tensor(name, list(shape), dt, kind="ExternalOutput").ap()

    def dint(name, shape, dt=F32):
        return nc.dram_tensor(name, list(shape), dt, kind="Internal").ap()

    xp = din("xp", [NSEQ, SEQ, D])
    g_mix = din("g_mix", [D]); w_in = din("w_in", [D, D_IN])
    g_sgu = din("g_sgu", [D_A]); w_sp = din("w_sp", [NH, 128, 128]); b_sp = din("b_sp", [NH, 128])
    g_q = din("g_q", [QR]); w_uq = din("w_uq", [QR, NH * 96])
    g_kv = din("g_kv", [KVR]); w_uk = din("w_uk", [KVR, NH * 64]); w_uv = din("w_uv", [KVR, NH * 64])
    g_oa = din("g_oa", [D_A]); g_ob = din("g_ob", [D_A]); w_out = din("w_out", [D, D])
    g_ffn = din("g_ffn", [D]); w_up = din("w_up", [D, 2 * D_FF]); w_conv = din("w_conv", [3, 2 * D_FF])
    b_conv = din("b_conv", [2 * D_FF]); w_down = din("w_down", [D_FF, D]); g_fin = din("g_fin", [D])
    c_ident = din("c_ident", [128, 128])
    c_mask = din("c_mask", [128, 128])
    c_cosp = din("c_cosp", [SEQ, HALF]); c_sinp = din("c_sinp", [SEQ, HALF])
    yp = dout("yp", [NSEQ, SEQ, D])
    o_plat = dout("o_plat", [NSEQ, SEQ, KVR])
    o_pkr = dout("o_pkr", [NSEQ, SEQ, ROPE])
    o_pconv = dout("o_pconv", [NSEQ, 2, 2 * D_FF])
    if cfg.debug:
        dbg_cs = dout("dbg_cs", [SEQ, 2 * HALF]); dbg_krp = dout("dbg_krp", [SEQ, ROPE])
    if cfg.do_sample:
        xs = din("xs", [NS, D])
        lat = din("lat", [NPL, 128, KVR]); krc = din("krc", [NPL, 128, ROPE])
        st = din("st", [NS, 2, 2 * D_FF])
        ptab = din("ptab", [NS, NPAGES], I32)
        c_coss = din("c_coss", [NS, HALF]); c_sins = din("c_sins", [NS, HALF])
        c_pgbase = din("c_pgbase", [128, 1])
        c_iota = din("c_iota", [128, 128]); c_glo = din("c_glo", [128, NGRP])
        accd_in = dint("accd_in", [128, NH * 257]); accd_out = dint("accd_out", [128, NH * 257])
        ys = dout("ys", [NS, D])
        if cfg.debug:
            dbg_acc = dout("dbg_acc", [128, NH * 257]); dbg_E = dout("dbg_E", [128, NGRP * 128])
            dbg_q = dout("dbg_q", [128, NH * 288]); dbg_en = dout("dbg_en", [128, NH])
        o_slat = dout("o_slat", [NS, KVR]); o_skr = dout("o_skr", [NS, ROPE])
        o_sv = dout("o_sv", [NS, D_A]); o_sconv = dout("o_sconv", [NS, 2, 2 * D_FF])
    wup_s = dint("wup_s", [NFT, 128, 2 * KD * 128], BF16)
    wdn_s = dint("wdn_s", [NFT // 2, 128, 2 * D], BF16)

    resid = [0]

    def sbr(name, shape, dt=F32):
        n = 1
        for d_ in shape[1:]:
            n *= d_
        resid[0] += (n * (2 if dt == BF16 else 4) + 63) // 64 * 64
        return es.enter_context(nc.sbuf_tensor(name, list(shape), dt))

    arena = {"t": None, "off": 0, "size": 0}

    def sb(name, shape, dt=F32):
        n = 1
        for d_ in shape[1:]:
            n *= d_
        words = n if dt != BF16 else (n + 1) // 2
        words = (words + 7) // 8 * 8
        off = arena["off"]
        assert off + words <= arena["size"], ("arena overflow", name, off, words, arena["size"])
        arena["off"] = off + words
        ap = arena["t"][0:shape[0], off:off + words]
        if dt != F32:
            ap = ap.bitcast(dt)
        ap = ap[:, 0:n]
        if len(shape) == 3:
            ap = ap.rearrange("p (a b) -> p a b", a=shape[1])
        elif len(shape) == 4:
            ap = ap.rearrange("p (a b c) -> p a b c", a=shape[1], b=shape[2])
        return ap

    ps = es.enter_context(nc.psum_tensor("ps", [128, 8, 512], F32))
    psctr = [0]
    psn = [6]

    def bank(n=1):
        b = psctr[0]
        if n == 2 and b % 2:
            b += 1
        if b + n > psn[0]:
            b = 0
        psctr[0] = (b + n) % psn[0]
        return b

    def pk(b, n=1):
        return ["ps%d" % (b + i) for i in range(n)]

    def dma(out, in_, reads, writes, semkey, is_out=False, eng="sp", slow=False):
        kw = dict(allow_slow_non_contiguous=True) if slow else {}
        P.add(eng, lambda e: e.dma_start(out=out, in_=in_, **kw), reads, writes, dma=True,
              semkey=semkey, is_out=is_out)

    def mm(out, lhsT, rhs, start, stop, reads, writes, skip=False):
        P.add("pe", lambda e: e.matmul(out, lhsT, rhs, start=start, stop=stop, skip_group_check=skip),
              reads, writes)

    def tr(out, in_, ident, reads, writes):
        P.add("pe", lambda e: e.transpose(out, in_, ident), reads, writes)

    def act(out, in_, func, reads, writes, scale=None, bias=None, accum=None):
        kw = {}
        if scale is not None:
            kw["scale"] = scale
        if bias is not None:
            kw["bias"] = bias
        if accum is not None:
            kw["accum_out"] = accum
        P.add("act", lambda e: e.activation(out, in_, func, **kw), reads, writes)

    def tt(eng, out, a, b, op, reads, writes):
        P.add(eng, lambda e: e.tensor_tensor(out, a, b, op), reads, writes)

    def ts(eng, out, a, s1, s2, op0, op1, reads, writes):
        if op1 is None:
            P.add(eng, lambda e: e.tensor_scalar(out, a, s1, None, op0), reads, writes)
        else:
            P.add(eng, lambda e: e.tensor_scalar(out, a, s1, s2, op0, op1), reads, writes)

    def stt(out, a, s, b, op0, op1, reads, writes):
        P.add("dve", lambda e: e.scalar_tensor_tensor(out, a, s, b, op0, op1), reads, writes)

    def cp(eng, out, in_, reads, writes):
        if eng == "act":
            P.add("act", lambda e: e.copy(out, in_), reads, writes)
        else:
            P.add(eng, lambda e: e.tensor_copy(out, in_), reads, writes)

    def recip(out, in_, reads, writes):
        P.add("dve", lambda e: e.reciprocal(out, in_), reads, writes)

    def memset(eng, ap, v, writes):
        P.add(eng, lambda e: e.memset(ap, v), (), writes)

    ident_f = sbr("ident_f", [128, 128]); ident_b = sbr("ident_b", [128, 128], BF16)
    maskT = sbr("maskT", [128, 128], BF16)
    eps_t = sbr("eps_t", [128, 1])
    w_in_b = sbr("w_in_b", [128, KD, D_IN], BF16)
    w_uq_b = sbr("w_uq_b", [128, 3, NH * 96], BF16)
    w_uk_b = sbr("w_uk_b", [128, 2, 512], BF16)
    w_uv_b = sbr("w_uv_b", [128, 2, 512], BF16)
    w_out_b = sbr("w_out_b", [128, KD, D], BF16)
    wsT = sbr("wsT", [128, NH, 128], BF16)
    bspT = sbr("bspT", [128, NH])
    Gsgu = sbr("Gsgu", [128, D_A]); Gkv = sbr("Gkv", [128, KVR]); Gfin = sbr("Gfin", [128, D])
    gk = sbr("gk", [128, 4, KD])
    wc = sbr("wc", [128, 3, 2 * NFT]); bc = sbr("bc", [128, 2 * NFT])
    if cfg.do_sample:
        w_ukT = sbr("w_ukT", [64, NH, KVR], BF16)
        W0 = sbr("W0", [128, NH]); B0 = sbr("B0", [128, NH])
        s_mix = sbr("s_mix", [128, D_A], BF16)
        acc_sb = sbr("acc_sb", [128, NH, 257])
        e_new = sbr("e_new", [128, NH])
        cnew_f = sbr("cnew_f", [128, KVR + ROPE])
        pgbase = sbr("pgbase", [128, 1]); iota_r = sbr("iota_r", [128, 128]); glo = sbr("glo", [128, NGRP])
        ones_b = sbr("ones_b", [128, 1], BF16)
    SBUF_BYTES = 229344
    arena["size"] = (SBUF_BYTES - 16384 - resid[0] - cfg.reserve) // 4 // 8 * 8
    arena["t"] = es.enter_context(nc.sbuf_tensor("arena", [128, arena["size"]], F32))
    stage = [sb("stage%d" % i, [128, 2048]) for i in range(2)]
    stg_b = [sb("stgb%d" % i, [128, 2048], BF16) for i in range(2)]

    dma(ident_f[:], c_ident, [], ["ident_f"], "ident_f")
    cp("dve", ident_b[:], ident_f[:], ["ident_f"], ["ident_b"])
    dma(stage[0][:, 0:128], c_mask, [], ["stage0"], "stage0")
    cp("dve", maskT[:], stage[0][:, 0:128], ["stage0"], ["maskT"])
    memset("dve", eps_t[:], EPS, ["eps_t"])
    dma(gk[:, 0, :], g_mix.rearrange("(k p) -> p k", p=128), [], ["gk"], "gk", slow=True)
    dma(gk[:, 1, 0:3], g_q.rearrange("(k p) -> p k", p=128), [], ["gk"], "gk", slow=True)
    dma(gk[:, 2, 0:4], g_oa.rearrange("(k p) -> p k", p=128), [], ["gk"], "gk", slow=True)
    dma(gk[:, 2, 4:8], g_ob.rearrange("(k p) -> p k", p=128), [], ["gk"], "gk", slow=True)
    dma(gk[:, 3, :], g_ffn.rearrange("(k p) -> p k", p=128), [], ["gk"], "gk", slow=True)
    dma(Gsgu[:], g_sgu.partition_broadcast(128), [], ["Gsgu"], "Gsgu")
    dma(Gkv[:], g_kv.partition_broadcast(128), [], ["Gkv"], "Gkv")
    dma(Gfin[:], g_fin.partition_broadcast(128), [], ["Gfin"], "Gfin")
    for k in range(3):
        dma(wc[:, k, :], w_conv[k].rearrange("(f p) -> p f", p=128), [], ["wc"], "wc", slow=True)
    dma(bc[:], b_conv.rearrange("(f p) -> p f", p=128), [], ["bc"], "bc", slow=True)
    dma(bspT[:], b_sp.rearrange("h t -> t h"), [], ["bspT"], "bspT", slow=True)

    wrr = [0]
    ceng = ["dve", "pool", "act"]

    def prep(dst_fn, src_ap, ncols, gain_ap, dst_keys, to_dram=None, store_view=None):
        i = wrr[0] % 2
        wrr[0] += 1
        sk = "stage%d" % i
        dma(stage[i][:, 0:ncols], src_ap, [], [sk], sk)
        eng = ceng[wrr[0] % 3]
        if to_dram is None:
            dst = dst_fn
            wk = dst_keys
        else:
            dst = stg_b[i][:, 0:ncols]
            wk = ["stgb%d" % i]
        if gain_ap is None:
            cp(eng, dst, stage[i][:, 0:ncols], [sk], wk)
        elif eng == "act":
            act(dst, stage[i][:, 0:ncols], AF.Copy, [sk, "gk"], wk, scale=gain_ap)
        else:
            ts(eng, dst, stage[i][:, 0:ncols], gain_ap, None, ALU.mult, None, [sk, "gk"], wk)
        if to_dram is not None:
            srcv = stg_b[i][:, 0:ncols]
            if store_view is not None:
                srcv = store_view(srcv)
            dma(to_dram, srcv, wk, dst_keys, "stgb%d" % i)

    for kt in range(KD):
        prep(w_in_b[:, kt, :], w_in[kt * 128:(kt + 1) * 128, :], D_IN, gk[:, 0, kt:kt + 1], ["w_in_b"])
    for kt in range(3):
        prep(w_uq_b[:, kt, :], w_uq[kt * 128:(kt + 1) * 128, :], NH * 96, gk[:, 1, kt:kt + 1], ["w_uq_b"])
    for kt in range(2):
        prep(w_uk_b[:, kt, :], w_uk[kt * 128:(kt + 1) * 128, :], 512, None, ["w_uk_b"])
        prep(w_uv_b[:, kt, :], w_uv[kt * 128:(kt + 1) * 128, :], 512, None, ["w_uv_b"])
    for kt in range(KD):
        prep(w_out_b[:, kt, :], w_out[kt * 128:(kt + 1) * 128, :], D, gk[:, 2, kt:kt + 1], ["w_out_b"])
    wup_v = wup_s.rearrange("ft p (gv kt f) -> ft p gv kt f", gv=2, kt=KD)
    for kt in range(KD):
        for gv in range(2):
            for half in range(2):
                c0 = gv * D_FF + half * 1408
                f0 = half * 11
                src = w_up[kt * 128:(kt + 1) * 128, c0:c0 + 1408]
                dstd = wup_v[f0:f0 + 11, :, gv, kt, :].rearrange("ft p f -> p ft f")
                prep(None, src, 1408, gk[:, 3, kt:kt + 1], ["wup_s"], to_dram=dstd,
                     store_view=lambda a: a.rearrange("p (ft f) -> p ft f", ft=11))
    wdn_v = wdn_s.rearrange("g p (two n) -> g p two n", two=2)
    for kt in range(NFT):
        prep(None, w_down[kt * 128:(kt + 1) * 128, :], D, None, ["wdn_s"],
             to_dram=wdn_v[kt // 2, :, kt % 2, :])
    for h in range(NH):
        i = wrr[0] % 2
        wrr[0] += 1
        sk = "stage%d" % i
        dma(stage[i][:, 0:128], w_sp[h], [], [sk], sk)
        b = bank()
        tr(ps[:, b, 0:128], stage[i][:, 0:128], ident_f[:], [sk, "ident_f"], pk(b))
        tt("dve", wsT[:, h, :], ps[:, b, 0:128], maskT[:], ALU.mult, pk(b) + ["maskT"], ["wsT"])

    if cfg.do_sample:
        for h in range(NH):
            if h % 4 == 0:
                bw = bank()
                pbw = ps[:, bw, :].bitcast(BF16)
            for kt in range(2):
                col = ((h % 4) * 2 + kt) * 128
                tr(pbw[0:64, col:col + 128], w_uk_b[:, kt, h * 64:(h + 1) * 64], ident_b[:],
                   ["w_uk_b", "ident_b"], pk(bw))
            if h % 4 == 3:
                cp("act", w_ukT[:, h - 3:h + 1, :], pbw[0:64, :].rearrange("p (h r) -> p h r", h=4), pk(bw),
                   ["w_ukT"])
        dma(W0[:], w_sp[:, 0, 0].partition_broadcast(128), [], ["W0"], "W0", slow=True)
        dma(B0[:], b_sp[:, 0].partition_broadcast(128), [], ["B0"], "B0", slow=True)
        dma(pgbase[:], c_pgbase, [], ["pgbase"], "pgbase")
        dma(iota_r[:], c_iota, [], ["iota_r"], "iota_r")
        dma(glo[:], c_glo, [], ["glo"], "glo")
        memset("dve", ones_b[:], 1.0, ["ones_b"])

    class NS_:
        pass
    B = NS_()

    def alloc_front():
        B.x2 = sb("x2", [128, NSUB, D])
        B.xT = sb("xT", [128, KD, 128], BF16)
        B.junk = sb("junk", [128, D], BF16)
        B.u_sb = sb("u_sb", [128, D_A]); B.vg = sb("vg", [128, D_A]); B.v_bf = sb("v_bf", [128, D_A], BF16)
        B.cq_bf = sb("cq_bf", [128, QR], BF16); B.cqT = sb("cqT", [128, 3, 128], BF16)
        B.ckv = sb("ckv", [128, KVR]); B.ckv_bf = sb("ckv_bf", [128, KVR], BF16)
        B.ckvT = sb("ckvT", [128, 2, 128], BF16)
        B.krp = sb("krp", [128, ROPE]); B.krr = sb("krr", [128, ROPE])
        B.qs = sb("qs", [128, NH, 96], BF16); B.qf = sb("qf", [128, NH, 96])
        B.ks = sb("ks", [128, NH, 96], BF16)
        B.cs = sb("cs", [128, 2, HALF])
        B.sm = sb("sm", [128, 16])
        B.rtmp = sb("rtmp", [128, NH, 2, HALF]); B.rtk = sb("rtk", [128, 2, HALF])
        B.gate_t = sb("gate_t", [128, D_A])

    def alloc_main():
        alloc_front()
        B.kT = sb("kT", [96, NH, SEQ], BF16)
        B.Vaug = sb("Vaug", [128, NKT, NH, 65], BF16)
        B.qT = sb("qT", [96, NH, TB], BF16)
        B.attn_o = sb("attn_o", [128, NSUB, D_A])
        B.mixbf = sb("mixbf", [128, NSUB, D], BF16)
        B.mT = sb("mT", [128, KD, 128], BF16)
        B.h2T = sb("h2T", [128, KD, TB + 2], BF16)
        B.actT = sb("actT", [128, NFT, TB], BF16)
        B.ptile = [sb("pt%d" % i, [128, TB], BF16) for i in range(3)]
        B.o_sb = sb("o_sb", [128, NSUB, 65]); B.rden = sb("rden", [128, NSUB])
        B.wu = [sb("wu%d" % i, [128, 2, KD, 128], BF16) for i in range(2)]
        B.wd = [sb("wd%d" % i, [128, 2, D], BF16) for i in range(2)]
        B.upg = sb("upg", [128, TB + 2]); B.upv = sb("upv", [128, TB + 2])
        B.cg = sb("cg", [128, TB]); B.cv = sb("cv", [128, TB])
        B.halo = sb("halo", [128, 2 * NFT, 2])
        if cfg.do_sample:
            B.stf = B.qf[:].rearrange("p h d -> p (h d)")[:, 0:512].rearrange("p (a b c) -> p a b c", a=2, b=2)
            B.upn = B.gate_t[:, 0:256].rearrange("p (g f) -> p g f", g=2)

    rr = dict(xt=0, pt=0, wu=0, wd=0, yt=0, blk=0)

    def rms_scale(dst, ss_ap, n, reads, writes):
        act(dst, ss_ap, AF.Sqrt, reads + ["eps_t"], writes, scale=1.0 / n, bias=eps_t[:])
        recip(dst, dst, writes, writes)

    def front(x_src, cos_src, sin_src, is_sample, sub=0, tok0=0, lat_out=None, kr_out=None, v_out=None):
        xk = "x2_%d" % sub
        xtile = B.x2[:, sub, :]
        dma(xtile, x_src, [], [xk], xk)
        dma(B.cs[:, 0, :], cos_src, [], ["cs"], "cs")
        dma(B.cs[:, 1, :], sin_src, [], ["cs"], "cs")
        act(B.junk[:], xtile, AF.Square, [xk], ["junk", "sm0"], accum=B.sm[:, 0:1])
        rms_scale(B.sm[:, 0:1], B.sm[:, 0:1], D, ["sm0"], ["sm0"])
        r = B.sm[:, 0:1]
        b = bank(2)
        for kt in range(KD):
            tr(ps[:, b + kt // 4, (kt % 4) * 128:(kt % 4 + 1) * 128], xtile[:, kt * 128:(kt + 1) * 128],
               ident_f[:], [xk, "ident_f"], pk(b + kt // 4))
        cp("dve", B.xT[:, 0:4, :], ps[:, b, :].rearrange("p (k t) -> p k t", k=4), pk(b), ["xTa"])
        cp("act", B.xT[:, 4:8, :], ps[:, b + 1, :].rearrange("p (k t) -> p k t", k=4), pk(b + 1), ["xTb"])
        bu, bv, bq, bk = bank(), bank(), bank(), bank()
        for (bb, c0, n) in ((bu, 0, 512), (bv, 512, 512), (bq, 1024, QR), (bk, 1024 + QR, KVR + ROPE)):
            for kt in range(KD):
                mm(ps[:, bb, 0:n], B.xT[:, kt, :], w_in_b[:, kt, c0:c0 + n], kt == 0, kt == KD - 1,
                   ["xTa", "xTb", "w_in_b"], pk(bb))
        act(B.u_sb[:], ps[:, bu, :], AF.Gelu, pk(bu) + ["sm0"], ["u_sb"], scale=r)
        act(B.vg[:], ps[:, bv, :], AF.Gelu, pk(bv) + ["sm0"], ["vg"], scale=r)
        act(B.junk[:, 0:D_A], B.vg[:], AF.Square, ["vg"], ["junk", "sm1"], accum=B.sm[:, 1:2])
        rms_scale(B.sm[:, 1:2], B.sm[:, 1:2], D_A, ["sm1"], ["sm1"])
        stt(B.vg[:], B.vg[:], B.sm[:, 1:2], Gsgu[:], ALU.mult, ALU.mult, ["vg", "sm1", "Gsgu"], ["vg"])
        if v_out is not None:
            dma(v_out, B.vg[:], ["vg"], [], "vg", is_out=True)
        act(B.junk[:, 0:QR], ps[:, bq, 0:QR], AF.Square, pk(bq) + ["sm0"], ["junk", "sm2"], scale=r,
            accum=B.sm[:, 2:3])
        rms_scale(B.sm[:, 2:3], B.sm[:, 2:3], QR, ["sm2"], ["sm2"])
        ts("dve", B.cq_bf[:], ps[:, bq, 0:QR], r, B.sm[:, 2:3], ALU.mult, ALU.mult, pk(bq) + ["sm0", "sm2"],
           ["cq_bf"])
        act(B.junk[:, 0:KVR], ps[:, bk, 0:KVR], AF.Square, pk(bk) + ["sm0"], ["junk", "sm3"], scale=r,
            accum=B.sm[:, 3:4])
        rms_scale(B.sm[:, 3:4], B.sm[:, 3:4], KVR, ["sm3"], ["sm3"])
        ts("dve", B.ckv[:], ps[:, bk, 0:KVR], r, B.sm[:, 3:4], ALU.mult, ALU.mult, pk(bk) + ["sm0", "sm3"],
           ["ckv"])
        tt("dve", B.ckv[:], B.ckv[:], Gkv[:], ALU.mult, ["ckv", "Gkv"], ["ckv"])
        cp("act", B.ckv_bf[:], B.ckv[:], ["ckv"], ["ckv_bf"])
        if lat_out is not None:
            dma(lat_out, B.ckv[:], ["ckv"], [], "ckv", is_out=True)
        ts("dve", B.krp[:], ps[:, bk, KVR:KVR + ROPE], r, None, ALU.mult, None, pk(bk) + ["sm0"], ["krp"])
        c_, s_ = B.cs[:, 0, :], B.cs[:, 1, :]
        if cfg.debug and not is_sample:
            dma(dbg_cs[tok0:tok0 + 128, :], B.cs[:].rearrange("p a b -> p (a b)"), ["cs"], [], "dbg1", is_out=True)
            dma(dbg_krp[tok0:tok0 + 128, :], B.krp[:], ["krp"], [], "dbg2", is_out=True)
        x1, x2_ = B.krp[:, 0:HALF], B.krp[:, HALF:ROPE]
        tt("dve", B.rtk[:, 0, :], x1, c_, ALU.mult, ["krp", "cs"], ["rtk"])
        tt("dve", B.rtk[:, 1, :], x2_, s_, ALU.mult, ["krp", "cs"], ["rtk"])
        tt("dve", B.krr[:, 0:HALF], B.rtk[:, 0, :], B.rtk[:, 1, :], ALU.subtract, ["rtk"], ["krr"])
        tt("dve", B.rtk[:, 0, :], x1, s_, ALU.mult, ["krp", "cs"], ["rtk"])
        tt("dve", B.rtk[:, 1, :], x2_, c_, ALU.mult, ["krp", "cs"], ["rtk"])
        tt("dve", B.krr[:, HALF:ROPE], B.rtk[:, 0, :], B.rtk[:, 1, :], ALU.add, ["rtk"], ["krr"])
        if kr_out is not None:
            dma(kr_out, B.krr[:], ["krr"], [], "krr", is_out=True)
        b = bank()
        pb = ps[:, b, :].bitcast(BF16)
        for kt in range(3):
            tr(pb[:, kt * 128:(kt + 1) * 128], B.cq_bf[:, kt * 128:(kt + 1) * 128], ident_b[:],
               ["cq_bf", "ident_b"], pk(b))
        cp("act", B.cqT[:], pb[:, 0:384].rearrange("p (k t) -> p k t", k=3), pk(b), ["cqT"])
        bq1, bq2 = bank(), bank()
        for (bb, c0) in ((bq1, 0), (bq2, 384)):
            for kt in range(3):
                mm(ps[:, bb, 0:384], B.cqT[:, kt, :], w_uq_b[:, kt, c0:c0 + 384], kt == 0, kt == 2,
                   ["cqT", "w_uq_b"], pk(bb))
        cp("act", B.qf[:, 0:4, :], ps[:, bq1, 0:384].rearrange("p (h d) -> p h d", h=4), pk(bq1), ["qf"])
        cp("dve", B.qf[:, 4:8, :], ps[:, bq2, 0:384].rearrange("p (h d) -> p h d", h=4), pk(bq2), ["qf"])
        cp("act", B.qs[:, :, 0:64], B.qf[:, :, 0:64], ["qf"], ["qs"])
        cb = B.cs[:, 0:1, :].to_broadcast([128, NH, HALF])
        sbb = B.cs[:, 1:2, :].to_broadcast([128, NH, HALF])
        q1, q2 = B.qf[:, :, 64:80], B.qf[:, :, 80:96]
        tt("dve", B.rtmp[:, :, 0, :], q1, cb, ALU.mult, ["qf", "cs"], ["rtmp"])
        tt("dve", B.rtmp[:, :, 1, :], q2, sbb, ALU.mult, ["qf", "cs"], ["rtmp1"])
        tt("dve", B.qs[:, :, 64:80], B.rtmp[:, :, 0, :], B.rtmp[:, :, 1, :], ALU.subtract, ["rtmp", "rtmp1"], ["qs"])
        tt("dve", B.rtmp[:, :, 0, :], q1, sbb, ALU.mult, ["qf", "cs"], ["rtmp"])
        tt("dve", B.rtmp[:, :, 1, :], q2, cb, ALU.mult, ["qf", "cs"], ["rtmp1"])
        tt("dve", B.qs[:, :, 80:96], B.rtmp[:, :, 0, :], B.rtmp[:, :, 1, :], ALU.add, ["rtmp", "rtmp1"], ["qs"])
        b = bank()
        pb = ps[:, b, :].bitcast(BF16)
        for kt in range(2):
            tr(pb[:, kt * 128:(kt + 1) * 128], B.ckv_bf[:, kt * 128:(kt + 1) * 128], ident_b[:],
               ["ckv_bf", "ident_b"], pk(b))
        cp("act", B.ckvT[:], pb[:, 0:256].rearrange("p (k t) -> p k t", k=2), pk(b), ["ckvT"])
        if is_sample:
            return
        b = bank()
        pb = ps[:, b, :].bitcast(BF16)
        for h in range(NH):
            tr(pb[0:96, h * 128:(h + 1) * 128], B.qs[:, h, :], ident_b[:], ["qs", "ident_b"], pk(b))
        cp("act", B.qT[:, :, sub * 128:(sub + 1) * 128], pb[0:96, :].rearrange("p (h t) -> p h t", h=NH),
           pk(b), ["qT"])
        bkn, bvl = bank(), bank()
        for (bb, w) in ((bkn, w_uk_b), (bvl, w_uv_b)):
            for kt in range(2):
                mm(ps[:, bb, :], B.ckvT[:, kt, :], w[:, kt, :], kt == 0, kt == 1, ["ckvT", "w_uk_b", "w_uv_b"],
                   pk(bb))
        cp("dve", B.ks[:, :, 0:64], ps[:, bkn, :].rearrange("p (h d) -> p h d", h=NH), pk(bkn), ["ks"])
        cp("act", B.ks[:, :, 64:96], B.krr[:].unsqueeze(1).to_broadcast([128, NH, ROPE]), ["krr"], ["ks"])
        ktile = tok0 // 128
        cp("act", B.Vaug[:, ktile, :, 0:64], ps[:, bvl, :].rearrange("p (h d) -> p h d", h=NH), pk(bvl),
           ["Vaug"])
        b = bank()
        pb = ps[:, b, :].bitcast(BF16)
        for h in range(NH):
            tr(pb[0:96, h * 128:(h + 1) * 128], B.ks[:, h, :], ident_b[:], ["ks", "ident_b"], pk(b))
        cp("dve", B.kT[:, :, tok0:tok0 + 128], pb[0:96, :].rearrange("p (h t) -> p h t", h=NH), pk(b), ["kT"])
        cp("act", B.v_bf[:], B.vg[:], ["vg"], ["v_bf"])
        b = bank()
        for h in range(NH):
            mm(ps[:, b, h * 64:(h + 1) * 64], wsT[:, h, :], B.v_bf[:, h * 64:(h + 1) * 64], True, True,
               ["wsT", "v_bf"], pk(b))
        tt("dve", B.gate_t[:].rearrange("p (h d) -> p h d", h=NH),
           ps[:, b, :].rearrange("p (h d) -> p h d", h=NH),
           bspT[:].unsqueeze(2).to_broadcast([128, NH, 64]), ALU.add, pk(b) + ["bspT"], ["gate_t"])
        tt("dve", B.gate_t[:], B.gate_t[:], B.u_sb[:], ALU.mult, ["gate_t", "u_sb"], ["gate_t"])
        act(B.junk[:, 0:D_A], B.gate_t[:], AF.Square, ["gate_t"], ["junk", "sm4"], accum=B.sm[:, 4:5])
        rms_scale(B.sm[:, 4:5], B.sm[:, 4:5], D_A, ["sm4"], ["sm4"])
        ts("dve", B.mixbf[:, sub, 0:D_A], B.gate_t[:], B.sm[:, 4:5], None, ALU.mult, None, ["gate_t", "sm4"],
           ["mixbf%d" % sub])

    def attention(blk):
        nkt = NSUB * blk + NSUB
        for hp in range(NH // 2):
            for kt in range(nkt):
                i = kt - NSUB * blk
                c0 = 128 * i if i > 0 else 0
                n = TB - c0
                for h in (2 * hp, 2 * hp + 1):
                    bo = 6 + (h % 2)
                    bs = bank()
                    mm(ps[:, bs, 0:n], B.kT[:, h, kt * 128:(kt + 1) * 128], B.qT[:, h, c0:TB], True, True,
                       ["kT", "qT"], pk(bs))
                    pi = rr["pt"] % 3
                    rr["pt"] += 1
                    pkey = "pt%d" % pi
                    pt_ = B.ptile[pi]
                    act(pt_[:, 0:n], ps[:, bs, 0:n], AF.Exp, pk(bs), [pkey], scale=ATTN_SCALE)
                    if i >= 0:
                        tt("pool", pt_[:, 0:128], pt_[:, 0:128], maskT[:], ALU.mult, [pkey, "maskT"], [pkey])
                    for j in range(max(i, 0), NSUB):
                        cj = (j * 128) - c0
                        mm(ps[:, bo, j * 65:(j + 1) * 65], pt_[:, cj:cj + 128], B.Vaug[:, kt, h, :],
                           kt == 0 and j == 0, kt == NSUB * blk + j, [pkey, "Vaug"], pk(bo), skip=True)
            for h in (2 * hp, 2 * hp + 1):
                bo = 6 + (h % 2)
                po = ps[:, bo, 0:NSUB * 65].rearrange("p (j d) -> p j d", j=NSUB)
                cp("act", B.o_sb[:], po, pk(bo), ["o_sb"])
                recip(B.rden[:], B.o_sb[:, :, 64], ["o_sb"], ["rden"])
                tt("dve", B.attn_o[:, :, h * 64:(h + 1) * 64], B.o_sb[:, :, 0:64],
                   B.rden[:].unsqueeze(2).to_broadcast([128, NSUB, 64]), ALU.mult, ["o_sb", "rden"], ["attn_o"])

    def post(sub, is_sample=False, mix_src=None):
        xk = None
        if not is_sample:
            act(B.junk[:, 0:D_A], B.attn_o[:, sub, :], AF.Square, ["attn_o"], ["junk", "sm5"], accum=B.sm[:, 5:6])
            rms_scale(B.sm[:, 5:6], B.sm[:, 5:6], D_A, ["sm5"], ["sm5"])
            ts("dve", B.mixbf[:, sub, D_A:D], B.attn_o[:, sub, :], B.sm[:, 5:6], None, ALU.mult, None,
               ["attn_o", "sm5"], ["mixbf%d" % sub])
        b = bank()
        pb = ps[:, b, :].bitcast(BF16)
        for kt in range(KD):
            tr(pb[:, kt * 128:(kt + 1) * 128], B.mixbf[:, sub, kt * 128:(kt + 1) * 128], ident_b[:],
               ["mixbf%d" % sub, "ident_b"], pk(b))
        cp("act", B.mT[:], pb.rearrange("p (k t) -> p k t", k=KD), pk(b), ["mT"])
        b = bank(2)
        for half in range(2):
            for kt in range(KD):
                mm(ps[:, b + half, :], B.mT[:, kt, :], w_out_b[:, kt, half * 512:(half + 1) * 512], kt == 0,
                   kt == KD - 1, ["mT", "w_out_b"], pk(b + half))
        return b

    def post2(sub, b, tcol):
        for half in range(2):
            tt("dve", B.x2[:, sub, half * 512:(half + 1) * 512], ps[:, b + half, :],
               B.x2[:, sub, half * 512:(half + 1) * 512], ALU.add, pk(b + half) + ["x2_%d" % sub],
               ["x2_%d" % sub])
        act(B.junk[:], B.x2[:, sub, :], AF.Square, ["x2_%d" % sub], ["junk", "sm6"], accum=B.sm[:, 6:7])
        rms_scale(B.sm[:, 6:7], B.sm[:, 6:7], D, ["sm6"], ["sm6"])
        act(B.mixbf[:, sub, :], B.x2[:, sub, :], AF.Copy, ["x2_%d" % sub, "sm6"], ["mixbf%d" % sub],
            scale=B.sm[:, 6:7])
        bb = bank()
        pb = ps[:, bb, :].bitcast(BF16)
        for kt in range(KD):
            tr(pb[:, kt * 128:(kt + 1) * 128], B.mixbf[:, sub, kt * 128:(kt + 1) * 128], ident_b[:],
               ["mixbf%d" % sub, "ident_b"], pk(bb))
        cp("act", B.h2T[:, :, 2 + tcol:2 + tcol + 128], pb.rearrange("p (k t) -> p k t", k=KD), pk(bb), ["h2T"])

    def ffn_up_sample():
        ntok = 128
        stv = st.rearrange("b j (gv f) -> b j gv f", gv=2)
        ocv = o_sconv[:, 1, :].rearrange("b (gv f) -> b gv f", gv=2)
        for ft in range(NFT):
            wi = rr["wu"] % 2
            rr["wu"] += 1
            wk = "wu%d" % wi
            dma(B.wu[wi][:].rearrange("p g k f -> p (g k f)"), wup_s[ft], ["wup_s"], [wk], wk)
            dma(B.stf[:], stv[:, :, :, ft * 128:(ft + 1) * 128], [], ["qf"], "qf")
            for gv, dst, dk in ((0, B.upg, "upg"), (1, B.upv, "upv")):
                b = bank()
                for kt in range(KD):
                    mm(ps[:, b, 0:ntok], B.wu[wi][:, gv, kt, :], B.h2T[:, kt, 2:2 + ntok], kt == 0, kt == KD - 1,
                       [wk, "h2T"], pk(b))
                cp("act", dst[:, 0:ntok], ps[:, b, 0:ntok], pk(b), [dk])
            bU = bank()
            for gv, src, dk, dst, ck in ((0, B.upg, "upg", B.cg, "cg"), (1, B.upv, "upv", B.cv, "cv")):
                f = gv * NFT + ft
                bT = bank()
                for j in range(2):
                    tr(ps[:, bT, j * 128:(j + 1) * 128], B.stf[:, j, gv, :], ident_f[:], ["qf", "ident_f"], pk(bT))
                act(dst[:, 0:ntok], src[:, 0:ntok], AF.Identity, [dk, "wc", "bc"], [ck], scale=wc[:, 2, f:f + 1],
                    bias=bc[:, f:f + 1])
                stt(dst[:, 0:ntok], ps[:, bT, 128:256], wc[:, 1, f:f + 1], dst[:, 0:ntok], ALU.mult, ALU.add,
                    pk(bT) + ["wc", ck], [ck])
                stt(dst[:, 0:ntok], ps[:, bT, 0:128], wc[:, 0, f:f + 1], dst[:, 0:ntok], ALU.mult, ALU.add,
                    pk(bT) + ["wc", ck], [ck])
                tr(ps[:, bU, gv * 128:(gv + 1) * 128], src[:, 0:ntok], ident_f[:], [dk, "ident_f"], pk(bU))
            cp("dve", B.upn[:], ps[:, bU, 0:256].rearrange("p (g f) -> p g f", g=2), pk(bU), ["gate_t"])
            dma(ocv[:, :, ft * 128:(ft + 1) * 128], B.upn[:], ["gate_t"], [], "gate_t", is_out=True)
            act(B.cg[:, 0:ntok], B.cg[:, 0:ntok], AF.Silu, ["cg"], ["cg"])
            tt("dve", B.actT[:, ft, 0:ntok], B.cg[:, 0:ntok], B.cv[:, 0:ntok], ALU.mult, ["cg", "cv"], ["actT"])

    def ffn_up(ntok, first_in_seq, st_fn=None, save_last=None):
        if first_in_seq:
            memset("pool", B.h2T[:, :, 0:2], 0.0, ["h2T"])
        N = ntok + 2
        for ft in range(NFT):
            wi = rr["wu"] % 2
            rr["wu"] += 1
            wk = "wu%d" % wi
            dma(B.wu[wi][:].rearrange("p g k f -> p (g k f)"), wup_s[ft], ["wup_s"], [wk], wk)
            for gv, dst, dk in ((0, B.upg, "upg"), (1, B.upv, "upv")):
                b = bank()
                for kt in range(KD):
                    mm(ps[:, b, 0:N], B.wu[wi][:, gv, kt, :], B.h2T[:, kt, 0:N], kt == 0, kt == KD - 1,
                       [wk, "h2T"], pk(b))
                cp("act", dst[:, 0:N], ps[:, b, 0:N], pk(b), [dk])
            for gv, src, dk, dst, ck in ((0, B.upg, "upg", B.cg, "cg"), (1, B.upv, "upv", B.cv, "cv")):
                f = gv * NFT + ft
                if st_fn is not None:
                    st_fn(ft, gv, src)
                act(dst[:, 0:ntok], src[:, 2:N], AF.Identity, [dk, "wc", "bc"], [ck], scale=wc[:, 2, f:f + 1],
                    bias=bc[:, f:f + 1])
                stt(dst[:, 0:ntok], src[:, 1:N - 1], wc[:, 1, f:f + 1], dst[:, 0:ntok], ALU.mult, ALU.add,
                    [dk, "wc", ck], [ck])
                stt(dst[:, 0:ntok], src[:, 0:N - 2], wc[:, 0, f:f + 1], dst[:, 0:ntok], ALU.mult, ALU.add,
                    [dk, "wc", ck], [ck])
                if save_last is not None:
                    cp("pool", B.halo[:, f, :], src[:, N - 2:N], [dk], ["halo"])
            act(B.cg[:, 0:ntok], B.cg[:, 0:ntok], AF.Silu, ["cg"], ["cg"])
            tt("dve", B.actT[:, ft, 0:ntok], B.cg[:, 0:ntok], B.cv[:, 0:ntok], ALU.mult, ["cg", "cv"], ["actT"])
        cp("pool", B.h2T[:, :, 0:2], B.h2T[:, :, ntok:ntok + 2], ["h2T"], ["h2T"])

    def ffn_down(nsub, out_fn):
        bs = [bank(2) for _ in range(nsub)]
        for g in range(NFT // 2):
            wi = rr["wd"] % 2
            rr["wd"] += 1
            wk = "wd%d" % wi
            dma(B.wd[wi][:].rearrange("p t n -> p (t n)"), wdn_s[g], ["wdn_s"], [wk], wk)
            for t2 in range(2):
                kt = 2 * g + t2
                for s_ in range(nsub):
                    for half in range(2):
                        mm(ps[:, bs[s_] + half, :], B.actT[:, kt, s_ * 128:(s_ + 1) * 128],
                           B.wd[wi][:, t2, half * 512:(half + 1) * 512], kt == 0, kt == NFT - 1,
                           [wk, "actT"], pk(bs[s_] + half))
        for s_ in range(nsub):
            out_fn(s_, bs[s_])

    def final_out(sub, b, dst_ap):
        yk = "x2_%d" % sub
        y = B.x2[:, sub, :]
        for half in range(2):
            tt("dve", y[:, half * 512:(half + 1) * 512], ps[:, b + half, :],
               y[:, half * 512:(half + 1) * 512], ALU.add, pk(b + half) + [yk], [yk])
        act(B.junk[:], y, AF.Square, [yk], ["junk", "sm7"], accum=B.sm[:, 7:8])
        rms_scale(B.sm[:, 7:8], B.sm[:, 7:8], D, ["sm7"], ["sm7"])
        stt(y, y, B.sm[:, 7:8], Gfin[:], ALU.mult, ALU.mult, [yk, "sm7", "Gfin"], [yk])
        dma(dst_ap, y, [yk], [], yk, is_out=True)


    if cfg.do_sample:
        P.barrier()
        arena["off"] = 0
        NPJ = NPAGES
        Qall = sb("Qall", [128, NH, KVR + ROPE], BF16)
        E_all = sb("E_all", [128, NGRP, 128], BF16); E_f = sb("E_f", [128, NGRP, 128])
        ET_all = sb("ET_all", [128, NGRP, 128], BF16)
        off_p = arena["off"]
        alloc_front()
        qnT = sb("qnT", [64, NH, 128], BF16)
        big = sb("big", [128, NH, KVR + ROPE])
        snew = sb("snew", [128, NH])
        pt_i = sb("pt_i", [128, NPJ], I32); pt_f = sb("pt_f", [128, NPJ])
        LT = sb("LT", [128, 128]); hi128 = sb("hi128", [128, 128]); lo_t = sb("lo_t", [128, 128])
        Gge = sb("Gge", [128, 128, NGRP]); Glt = sb("Glt", [128, 128, NGRP])
        G_bf = sb("G_bf", [128, 128, NGRP], BF16)
        RT = sb("RT", [128, 128, 128], BF16)

        front(xs, c_coss, c_sins, True, lat_out=o_slat, kr_out=o_skr, v_out=o_sv)
        g3 = B.gate_t[:].rearrange("p (h d) -> p h d", h=NH)
        tt("dve", g3, B.vg[:].rearrange("p (h d) -> p h d", h=NH), W0[:].unsqueeze(2).to_broadcast([128, NH, 64]),
           ALU.mult, ["vg", "W0"], ["gate_t"])
        tt("dve", g3, g3, B0[:].unsqueeze(2).to_broadcast([128, NH, 64]), ALU.add, ["gate_t", "B0"], ["gate_t"])
        tt("dve", B.gate_t[:], B.gate_t[:], B.u_sb[:], ALU.mult, ["gate_t", "u_sb"], ["gate_t"])
        act(B.junk[:, 0:D_A], B.gate_t[:], AF.Square, ["gate_t"], ["junk", "sm4"], accum=B.sm[:, 4:5])
        rms_scale(B.sm[:, 4:5], B.sm[:, 4:5], D_A, ["sm4"], ["sm4"])
        ts("dve", s_mix[:], B.gate_t[:], B.sm[:, 4:5], None, ALU.mult, None, ["gate_t", "sm4"], ["s_mix"])
        cp("pool", cnew_f[:, 0:KVR], B.ckv[:], ["ckv"], ["cnew_f"])
        cp("pool", cnew_f[:, KVR:KVR + ROPE], B.krr[:], ["krr"], ["cnew_f"])
        b = bank()
        pb = ps[:, b, :].bitcast(BF16)
        for h in range(NH):
            tr(pb[0:64, h * 128:(h + 1) * 128], B.qs[:, h, 0:64], ident_b[:], ["qs", "ident_b"], pk(b))
        cp("act", qnT[:], pb[0:64, :].rearrange("p (h t) -> p h t", h=NH), pk(b), ["qnT"])
        for h2 in range(NH // 2):
            bb = bank()
            for hh in range(2):
                h = 2 * h2 + hh
                mm(ps[:, bb, hh * KVR:(hh + 1) * KVR], qnT[:, h, :], w_ukT[:, h, :], True, True,
                   ["qnT", "w_ukT"], pk(bb))
            cp("act" if h2 % 2 else "dve", Qall[:, 2 * h2:2 * h2 + 2, 0:KVR],
               ps[:, bb, :].rearrange("p (h r) -> p h r", h=2), pk(bb), ["Qall"])
        cp("pool", Qall[:, :, KVR:KVR + ROPE], B.qs[:, :, 64:96], ["qs"], ["Qall"])
        tt("dve", big[:], Qall[:], cnew_f[:].unsqueeze(1).to_broadcast([128, NH, KVR + ROPE]), ALU.mult,
           ["Qall", "cnew_f"], ["big"])
        P.add("dve", lambda e: e.tensor_reduce(snew[:], big[:], AX.X, ALU.add), ["big"], ["snew"])
        act(e_new[:], snew[:], AF.Exp, ["snew"], ["e_new"], scale=ATTN_SCALE)

        try:
            if "E" not in cfg.stages:
                raise StopIteration
            dma(pt_i[:], ptab, [], ["pt_i"], "pt_i")
            cp("dve", pt_f[:], pt_i[:], ["pt_i"], ["pt_f"])
            ts("dve", pt_f[:], pt_f[:], pgbase[:, 0:1], None, ALU.subtract, None, ["pt_f", "pgbase"], ["pt_f"])
            b = bank()
            if cfg.estop < 1:
                raise StopIteration
            tr(ps[0:NPJ, b, 0:128], pt_f[:], ident_f[:], ["pt_f", "ident_f"], pk(b))
            cp("dve", LT[0:NPJ, :], ps[0:NPJ, b, 0:128], pk(b), ["LT"])
            LTb = LT[0:NPJ, :].unsqueeze(2).to_broadcast([NPJ, 128, NGRP])
            glob = glo[0:NPJ, :].unsqueeze(1).to_broadcast([NPJ, 128, NGRP])
            if cfg.estop < 2:
                raise StopIteration
            tt("dve", Gge[0:NPJ], LTb, glob, ALU.is_ge, ["LT", "glo"], ["Gge"])
            tt("dve", Glt[0:NPJ], LTb, glob, ALU.subtract, ["LT", "glo"], ["Glt"])
            ts("dve", Glt[0:NPJ], Glt[0:NPJ], 128.0, None, ALU.is_lt, None, ["Glt"], ["Glt"])
            tt("dve", Gge[0:NPJ], Gge[0:NPJ], Glt[0:NPJ], ALU.mult, ["Gge", "Glt"], ["Gge"])
            cp("dve", G_bf[0:NPJ], Gge[0:NPJ], ["Gge"], ["G_bf"])
            if cfg.estop < 3:
                raise StopIteration
            tt("dve", Glt[0:NPJ], Gge[0:NPJ], glob, ALU.mult, ["Gge", "glo"], ["Glt"])
            P.add("dve", lambda e: e.tensor_reduce(hi128[0:NPJ, :], Glt[0:NPJ], AX.X, ALU.add), ["Glt"], ["hi128"])
            tt("dve", lo_t[0:NPJ, :], LT[0:NPJ, :], hi128[0:NPJ, :], ALU.subtract, ["LT", "hi128"], ["lo_t"])
            if cfg.estop < 4:
                raise StopIteration
            tt("dve", RT[0:NPJ], lo_t[0:NPJ, :].unsqueeze(2).to_broadcast([NPJ, 128, 128]),
               iota_r[0:NPJ, :].unsqueeze(1).to_broadcast([NPJ, 128, 128]), ALU.is_equal, ["lo_t", "iota_r"], ["RT"])
            if cfg.estop < 5:
                raise StopIteration
            CB = 512 // NGRP
            for b0 in range(0, 128, CB):
                nb = min(CB, 128 - b0)
                bb = bank()
                for bi in range(nb):
                    mm(ps[:, bb, bi * NGRP:(bi + 1) * NGRP], RT[0:NPJ, b0 + bi, :], G_bf[0:NPJ, b0 + bi, :], True, True,
                       ["RT", "G_bf"], pk(bb))
                src = ps[:, bb, 0:nb * NGRP].rearrange("p (b g) -> p g b", g=NGRP)
                cp("act", E_all[:, :, b0:b0 + nb], src, pk(bb), ["E_all"])
                cp("dve", E_f[:, :, b0:b0 + nb], src, pk(bb), ["E_f"])
            if cfg.estop < 6:
                raise StopIteration
            for g0 in range(0, NGRP, 8):
                ng = min(8, NGRP - g0)
                bb = bank()
                pb = ps[:, bb, :].bitcast(BF16)
                for gi in range(ng):
                    tr(pb[:, gi * 128:(gi + 1) * 128], E_all[:, g0 + gi, :], ident_b[:], ["E_all", "ident_b"], pk(bb))
                cp("act", ET_all[:, g0:g0 + ng, :], pb[:, 0:ng * 128].rearrange("p (g t) -> p g t", g=ng), pk(bb),
                   ["ET_all"])
        except StopIteration:
            pass
        memset("pool", acc_sb[:], 0.0, ["acc_sb"])

        P.barrier()
        arena["off"] = off_p
        QselT = sb("QselT", [128, 3, 128, NH], BF16)
        Xf = [sb("Xf%d" % i, [128, 4, KVR + ROPE]) for i in range(3)]
        Xb = [sb("Xb%d" % i, [128, 4, KVR + ROPE], BF16) for i in range(8)]
        XT = [sb("XT%d" % i, [128, 3, 4, 128], BF16) for i in range(2)]
        et_all = [sb("et%d" % i, [128, 128, NH], BF16) for i in range(2)]
        OT_sb = sb("OT_sb", [128, 2, NH, 128])
        Opg = sb("Opg", [128, NH, 257])
        xrr = dict(xf=0, xb=0, xt=0, ev=0)
        psn[0] = 4
        psctr[0] = 0
        for g in range(NGRP if "loop" in cfg.stages else 0):
            for k, (c0, rows) in enumerate(((0, 128), (128, 128), (256, ROPE))):
                bq = bank(2)
                for h in range(NH):
                    mm(ps[0:rows, bq + h // 4, (h % 4) * 128:(h % 4 + 1) * 128], Qall[:, h, c0:c0 + rows],
                       ET_all[:, g, :], True, True, ["Qall", "ET_all"], pk(bq + h // 4))
                for hb in range(2):
                    cp("act" if hb else "dve", QselT[0:rows, k, :, hb * 4:(hb + 1) * 4],
                       ps[0:rows, bq + hb, :].rearrange("p (h q) -> p q h", h=4), pk(bq + hb), ["QselT"])
            et = et_all[g % 2]
            ek = "et%d" % (g % 2)
            for sbi in range(8):
                bS = 4 + sbi % 2
                slots = []
                for q4 in range(4):
                    pg0 = g * 128 + sbi * 16 + q4 * 4
                    si = xrr["xb"] % 8; xrr["xb"] += 1
                    bk_ = "Xb%d" % si
                    slots.append(si)
                    if cfg.swdge_cast:
                        dma(Xb[si][:, :, 0:KVR], lat[pg0:pg0 + 4].rearrange("n t r -> t n r"), [], [bk_], bk_,
                            eng="pool")
                        dma(Xb[si][:, :, KVR:KVR + ROPE], krc[pg0:pg0 + 4].rearrange("n t r -> t n r"), [], [bk_],
                            bk_, eng="pool")
                    else:
                        xi = xrr["xf"] % 3; xrr["xf"] += 1
                        xk = "Xf%d" % xi
                        dma(Xf[xi][:, :, 0:KVR], lat[pg0:pg0 + 4].rearrange("n t r -> t n r"), [], [xk], xk)
                        dma(Xf[xi][:, :, KVR:KVR + ROPE], krc[pg0:pg0 + 4].rearrange("n t r -> t n r"), [], [xk], xk)
                        if cfg.pool_cast:
                            cp("pool", Xb[si][:], Xf[xi][:], [xk], [bk_])
                        elif xrr["xb"] % 2:
                            cp("dve", Xb[si][:], Xf[xi][:], [xk], [bk_])
                        else:
                            act(Xb[si][:], Xf[xi][:], AF.Copy, [xk], [bk_])
                    bA, bB = bank(), bank()
                    pbA = ps[:, bA, :].bitcast(BF16)
                    pbB = ps[:, bB, :].bitcast(BF16)
                    for pg in range(4):
                        for c in range(2):
                            tr(pbA[:, (c * 4 + pg) * 128:(c * 4 + pg + 1) * 128], Xb[si][:, pg, c * 128:(c + 1) * 128],
                               ident_b[:], [bk_, "ident_b"], pk(bA))
                        tr(pbB[0:ROPE, pg * 128:(pg + 1) * 128], Xb[si][:, pg, KVR:KVR + ROPE], ident_b[:],
                           [bk_, "ident_b"], pk(bB))
                    ti = xrr["xt"] % 2; xrr["xt"] += 1
                    tk_ = "XT%d" % ti
                    ev = xrr["ev"] % 2; xrr["ev"] += 1
                    cp("act" if ev else "dve", XT[ti][:, 0:2, :, :],
                       pbA.rearrange("p (c n t) -> p c n t", c=2, n=4), pk(bA), [tk_ + "a"])
                    cp("dve" if ev else "act", XT[ti][0:ROPE, 2, :, :],
                       pbB[0:ROPE, 0:512].rearrange("p (n t) -> p n t", n=4), pk(bB), [tk_ + "b"])
                    for pg in range(4):
                        p = sbi * 16 + q4 * 4 + pg
                        col = (q4 * 4 + pg) * NH
                        for k, rows in enumerate((128, 128, ROPE)):
                            mm(ps[:, bS, col:col + NH], XT[ti][0:rows, k, pg, :], QselT[0:rows, k, p, :], k == 0,
                               k == 2, [tk_ + "a", tk_ + "b", "QselT"], pk(bS))
                act(et[:, sbi * 16:(sbi + 1) * 16, :].rearrange("p n h -> p (n h)"), ps[:, bS, 0:16 * NH], AF.Exp,
                    pk(bS), [ek], scale=ATTN_SCALE)
                hp = sbi % 4
                for q4 in range(4):
                    si = slots[q4]
                    for pg in range(4):
                        p = sbi * 16 + q4 * 4 + pg
                        pc = (p % 64) * NH
                        for c in range(2):
                            mm(ps[:, 6 + c, pc:pc + NH], Xb[si][:, pg, c * 128:(c + 1) * 128], et[:, p, :], True, True,
                               ["Xb%d" % si, ek], pk(6 + c))
                if sbi % 4 == 3:
                    half = sbi // 4
                    for c in range(2):
                        cp("act" if c else "dve", OT_sb[:, c, :, half * 64:(half + 1) * 64],
                           ps[:, 6 + c, :].rearrange("p (q h) -> p h q", h=NH), pk(6 + c), ["OT_sb"])
            bL = bank()
            for h in range(NH):
                mm(ps[:, bL, h:h + 1], et[:, :, h], ones_b[:], True, True, [ek, "ones_b"], pk(bL))
            cp("dve", Opg[:, :, 256], ps[:, bL, 0:NH], pk(bL), ["Opg"])
            for hq in range(2):
                bt = bank(2)
                for hh in range(4):
                    h = hq * 4 + hh
                    for c in range(2):
                        col = ((hh % 2) * 2 + c) * 128
                        tr(ps[:, bt + hh // 2, col:col + 128], OT_sb[:, c, h, :], ident_f[:], ["OT_sb", "ident_f"],
                           pk(bt + hh // 2))
                for i2 in range(2):
                    cp("act" if i2 else "dve", Opg[:, hq * 4 + 2 * i2:hq * 4 + 2 * i2 + 2, 0:256],
                       ps[:, bt + i2, :].rearrange("p (h r) -> p h r", h=2), pk(bt + i2), ["Opg"])
            opf = Opg[:].rearrange("p h r -> p (h r)")
            acf = acc_sb[:].rearrange("p h r -> p (h r)")
            for c0 in range(0, NH * 257, 512):
                n = min(512, NH * 257 - c0)
                bc_ = bank()
                mm(ps[:, bc_, 0:n], E_f[:, g, :], opf[:, c0:c0 + n], True, True, ["E_f", "Opg"], pk(bc_))
                tt("dve", acf[:, c0:c0 + n], acf[:, c0:c0 + n], ps[:, bc_, 0:n], ALU.add, pk(bc_) + ["acc_sb"],
                   ["acc_sb"])
        if cfg.debug:
            dma(dbg_acc, acc_sb[:].rearrange("p h r -> p (h r)"), ["acc_sb"], [], "dbg3", is_out=True)
            dma(dbg_E, E_f[:].rearrange("p g b -> p (g b)"), ["E_f"], [], "dbg4", is_out=True)
            dma(dbg_en, e_new[:], ["e_new"], [], "dbg6", is_out=True)
        psn[0] = 6
        psctr[0] = 0
        dma(accd_in, acc_sb[:].rearrange("p h r -> p (h r)"), ["acc_sb"], ["accd_in"], "accd")
        if cfg.ncores > 1 and "cc" in cfg.stages:
            P.add("pool", lambda e: e.collective_compute("AllReduce", ALU.add, replica_groups=[list(range(cfg.ncores))],
                                                         ins=[accd_in], outs=[accd_out]),
                  ["accd_in"], ["accd_out"], dma=True, semkey="cc", inc=1)
        else:
            dma(accd_out, accd_in, ["accd_in"], ["accd_out"], "cc")

    P.barrier()
    arena["off"] = 0
    psn[0] = 6
    psctr[0] = 0
    alloc_main()
    memset("pool", B.Vaug[:], 1.0, ["Vaug"])
    for sq in range(NSEQ if "prompt" in cfg.stages else 0):
        for blk in range(NBLK):
            t0 = blk * TB
            for sub in range(NSUB):
                tk = t0 + sub * 128
                front(xp[sq, tk:tk + 128, :], c_cosp[tk:tk + 128, :], c_sinp[tk:tk + 128, :], False, sub, tk,
                      lat_out=o_plat[sq, tk:tk + 128, :], kr_out=o_pkr[sq, tk:tk + 128, :])
            attention(blk)
            for sub in range(NSUB):
                b = post(sub)
                post2(sub, b, sub * 128)
            last = blk == NBLK - 1
            ffn_up(TB, blk == 0, save_last=True if last else None)
            if last:
                for j in range(2):
                    for gv in range(2):
                        dma(o_pconv[sq, j, gv * D_FF:(gv + 1) * D_FF].rearrange("(f p) -> p f", p=128),
                            B.halo[:, gv * NFT:(gv + 1) * NFT, j], ["halo"], [], "halo", is_out=True, slow=True)

            def outf(s_, b, sq=sq, t0=t0):
                final_out(s_, b, yp[sq, t0 + s_ * 128:t0 + (s_ + 1) * 128, :])
            ffn_down(NSUB, outf)


    if cfg.do_sample and "s3" in cfg.stages:
        acf = acc_sb[:].rearrange("p h r -> p (h r)")
        dma(acf, accd_out, ["accd_out"], ["acc_sb"], "accd")
        tmp3 = B.x2[:, 0:2, :].rearrange("p a (h r) -> p (a h) r", h=4)
        cb3 = cnew_f[:, 0:KVR].unsqueeze(1).to_broadcast([128, NH, KVR])
        eb3 = e_new[:].unsqueeze(2).to_broadcast([128, NH, KVR])
        tt("dve", tmp3, cb3, eb3, ALU.mult, ["cnew_f", "e_new"], ["x2_0", "x2_1"])
        tt("dve", acc_sb[:, :, 0:KVR], acc_sb[:, :, 0:KVR], tmp3, ALU.add, ["acc_sb", "x2_0", "x2_1"], ["acc_sb"])
        tt("dve", acc_sb[:, :, 256], acc_sb[:, :, 256], e_new[:], ALU.add, ["acc_sb", "e_new"], ["acc_sb"])
        rl = B.sm[:, 8:16]
        recip(rl, acc_sb[:, :, 256], ["acc_sb"], ["sm8"])
        olat_bf = B.mixbf[:, 0:2, :].rearrange("p a (h r) -> p (a h) r", h=4)
        tt("dve", olat_bf, acc_sb[:, :, 0:KVR], rl.unsqueeze(2).to_broadcast([128, NH, KVR]), ALU.mult,
           ["acc_sb", "sm8"], ["mixbf0", "mixbf1"])
        olT = B.actT[:, 0:NH, :]
        for hq in range(2):
            bb = bank()
            pb = ps[:, bb, :].bitcast(BF16)
            for hh in range(4):
                for c in range(2):
                    col = (hh * 2 + c) * 128
                    tr(pb[:, col:col + 128], olat_bf[:, hq * 4 + hh, c * 128:(c + 1) * 128], ident_b[:],
                       ["mixbf0", "mixbf1", "ident_b"], pk(bb))
            cp("act", olT[:, hq * 4:hq * 4 + 4, :], pb.rearrange("p (h x) -> p h x", h=4), pk(bb), ["actT"])
        bo_ = bank()
        for h in range(NH):
            for c in range(2):
                mm(ps[:, bo_, h * 64:(h + 1) * 64], olT[:, h, c * 128:(c + 1) * 128],
                   w_uv_b[:, c, h * 64:(h + 1) * 64], c == 0, c == 1, ["actT", "w_uv_b"], pk(bo_), skip=True)
        cp("act", B.attn_o[:, 0, :], ps[:, bo_, :], pk(bo_), ["attn_o"])
        cp("pool", B.mixbf[:, 0, 0:D_A], s_mix[:], ["s_mix"], ["mixbf0"])
        dma(B.x2[:, 0, :], xs, [], ["x2_0"], "x2_0")
        bpo = post(0)
        post2(0, bpo, 0)
        dma(o_sconv[:, 0, :], st[:, 1, :], [], [], "sconv0", is_out=True)
        ffn_up_sample()

        def outs(s_, b):
            final_out(s_, b, ys)
        ffn_down(1, outs)

    P.emit(nc, es)
    es.close()
    return nc, P


def rope_tables(pos):
    inv = (np.float32(10000.0) ** (-np.arange(HALF, dtype=np.float32) / np.float32(HALF))).astype(np.float32)
    ang = pos.astype(np.float32)[:, None] * inv[None, :]
    return np.cos(ang).astype(np.float32), np.sin(ang).astype(np.float32)


def make_in_maps(cfg, inputs):
    nseq, seq = cfg.nseq, cfg.seq
    f = lambda a: np.ascontiguousarray(np.asarray(a, dtype=np.float32))
    cosp, sinp = rope_tables(np.arange(seq))
    kk = np.arange(128)
    common = dict(
        g_mix=f(inputs["g_mix"][0]), w_in=f(inputs["w_in"][0]), g_sgu=f(inputs["g_sgu"][0]),
        w_sp=f(inputs["w_spatial"][0]), b_sp=f(inputs["b_spatial"][0]), g_q=f(inputs["g_q"][0]),
        w_uq=f(inputs["w_uq"][0]).reshape(QR, NH * 96), g_kv=f(inputs["g_kv"][0]),
        w_uk=f(inputs["w_uk"][0]).reshape(KVR, NH * 64), w_uv=f(inputs["w_uv"][0]).reshape(KVR, NH * 64),
        g_oa=f(inputs["g_out_a"][0]), g_ob=f(inputs["g_out_b"][0]), w_out=f(inputs["w_out"][0]),
        g_ffn=f(inputs["g_ffn"][0]), w_up=f(inputs["w_up"][0]), w_conv=f(inputs["w_conv"][0]),
        b_conv=f(inputs["b_conv"][0]), w_down=f(inputs["w_down"][0]), g_fin=f(inputs["g_final"]),
        c_ident=np.eye(128, dtype=np.float32),
        c_mask=(kk[:, None] <= kk[None, :]).astype(np.float32),
        c_cosp=cosp, c_sinp=sinp,
    )
    maps = []
    xpr = f(inputs["x_prompt"])
    if cfg.do_sample:
        npl, ngrp = cfg.npl, cfg.npl // 128
        past_len = cfg.npages * 128
        coss, sins = rope_tables(np.full((NS,), past_len))
        common.update(
            xs=f(inputs["x_sample"][:, 0, :]), st=f(inputs["state_ffn_conv"][0]),
            ptab=np.ascontiguousarray(np.asarray(inputs["page_table"], dtype=np.int32)),
            c_coss=coss, c_sins=sins,
            c_iota=np.tile(np.arange(128, dtype=np.float32)[None, :], (128, 1)),
            c_glo=np.tile((128.0 * np.arange(ngrp, dtype=np.float32))[None, :], (128, 1)),
        )
        latc = inputs["cache_kv_latent"][0]
        krcc = inputs["cache_k_rope"][0]
    for c in range(cfg.ncores):
        m = dict(common)
        m["xp"] = xpr[c * nseq:(c + 1) * nseq]
        if cfg.do_sample:
            m["lat"] = f(latc[c * npl:(c + 1) * npl])
            m["krc"] = f(krcc[c * npl:(c + 1) * npl])
            m["c_pgbase"] = np.full((128, 1), float(c * npl), dtype=np.float32)
        maps.append(m)
    return maps


_CACHE = {}


def kernel(**inputs):
    cfg = Cfg()
    if "nc" not in _CACHE:
        _CACHE["nc"] = build(cfg)[0]
    nc = _CACHE["nc"]
    maps = make_in_maps(cfg, inputs)
    res = run_bass_kernel_spmd(nc, maps, core_ids=list(range(cfg.ncores)))
    r = res.results
    cat = lambda k: np.concatenate([r[c][k] for c in range(cfg.ncores)], axis=0)
    y_prompt = cat("yp")
    p_lat = cat("o_plat")[None]
    p_kr = cat("o_pkr")[None]
    p_conv = cat("o_pconv")[None]
    y_sample = r[0]["ys"][:, None, :]
    s_lat = r[0]["o_slat"][None, :, None, :]
    s_kr = r[0]["o_skr"][None, :, None, :]
    s_v = r[0]["o_sv"][None, :, None, :]
    s_conv = r[0]["o_sconv"][None]
    return (y_prompt, y_sample, p_lat, p_kr, p_conv, s_lat, s_kr, s_v, s_conv)
```
